# Optimizing a Trainium2 kernel written in Bass

```python
import jax, jax.numpy as jnp
from jax import lax
import numpy as np

D_MODEL = 4096
BATCH = 8
SEQ = 2048
DEPTH = 2

N_META = 16
RET_HEADS = 8
D_RET = D_MODEL // 2
RET_HEAD_DIM = D_RET // RET_HEADS
D_CONV = D_MODEL // 2
CONV_GROUPS = 8
CONV_WIDTH = 3
D_FF = 4 * D_MODEL
CHUNK = 128
ROPE_BASE = 10000.0
EPS = 1e-6
IN_SIZES = (D_RET, D_RET, D_RET, D_RET, D_CONV, D_CONV, D_CONV, D_MODEL, D_MODEL)
D_IN = sum(IN_SIZES)

kernel_name = "hybrid_retention_shortconv_gated_block"


def rmsnorm(x, g):
    xf = x.astype(jnp.float32)
    y = xf * lax.rsqrt(jnp.mean(jnp.square(xf), axis=-1, keepdims=True) + EPS)
    return (y * g.astype(jnp.float32)).astype(x.dtype)


def rotary(t, pos):
    d = t.shape[-1]
    inv_freq = ROPE_BASE ** (-jnp.arange(0, d, 2, dtype=jnp.float32) / d)
    ang = pos[:, None] * inv_freq[None, :]
    cos, sin = jnp.cos(ang), jnp.sin(ang)
    t1, t2 = t[..., : d // 2], t[..., d // 2:]
    return jnp.concatenate([t1 * cos - t2 * sin, t1 * sin + t2 * cos], axis=-1)


def chunkwise_retention(q, k, v):
    b, h, l, dk = q.shape
    dv = v.shape[-1]
    pad = (-l) % CHUNK
    padw = ((0, 0), (0, 0), (pad, 0), (0, 0))
    q, k, v = jnp.pad(q, padw), jnp.pad(k, padw), jnp.pad(v, padw)
    n = (l + pad) // CHUNK

    def to_chunks(t):
        return t.reshape(b, h, n, CHUNK, t.shape[-1]).transpose(2, 0, 1, 3, 4)

    qc, kc, vc = to_chunks(q), to_chunks(k), to_chunks(v)
    gamma = 1.0 - jnp.exp2(-5.0 - jnp.arange(h, dtype=jnp.float32))
    log_g = jnp.log(gamma)
    idx = jnp.arange(CHUNK, dtype=jnp.float32)
    rel = idx[:, None] - idx[None, :]
    inner_decay = jnp.where(rel >= 0, jnp.exp(log_g[:, None, None] * jnp.maximum(rel, 0.0)), 0.0)
    q_decay = jnp.exp(log_g[:, None] * (idx[None, :] + 1.0))
    k_decay = jnp.exp(log_g[:, None] * (CHUNK - 1.0 - idx[None, :]))
    chunk_decay = jnp.exp(log_g * CHUNK)

    def step(state, inp):
        qi, ki, vi = inp
        scores = jnp.einsum('bhqd,bhkd->bhqk', qi, ki) * inner_decay[None]
        inner = jnp.einsum('bhqk,bhkv->bhqv', scores, vi)
        cross = jnp.einsum('bhqd,bhdv->bhqv', qi, state) * q_decay[None, :, :, None]
        new_state = state * chunk_decay[None, :, None, None] + jnp.einsum(
            'bhkd,bhkv->bhdv', ki * k_decay[None, :, :, None], vi)
        return new_state, inner + cross

    state0 = jnp.zeros((b, h, dk, dv), jnp.float32)
    _, out = lax.scan(step, state0, (qc, kc, vc))
    out = out.transpose(1, 2, 0, 3, 4).reshape(b, h, n * CHUNK, dv)
    return out[:, :, pad:]


def hybrid_layer(x, pos, g_mix, w_in, conv_w, w_ret_out, w_conv_out, w_out, g_mlp, w_up, w_down):
    b, l, _ = x.shape
    u = rmsnorm(x, g_mix)
    split_points = [int(s) for s in np.cumsum(IN_SIZES)[:-1]]
    q, k, v, g_ret, c_gate, b_gate, z, gate_a, gate_b = jnp.split(u @ w_in, split_points, axis=-1)

    def heads(t):
        return t.reshape(b, l, RET_HEADS, RET_HEAD_DIM).transpose(0, 2, 1, 3).astype(jnp.float32)

    qh = rotary(heads(q), pos)
    kh = rotary(heads(k), pos) * (RET_HEAD_DIM ** -0.5)
    vh = heads(v)
    ret = chunkwise_retention(qh, kh, vh)
    ret = ret * lax.rsqrt(jnp.mean(jnp.square(ret), axis=-1, keepdims=True) + EPS)
    ret = ret.transpose(0, 2, 1, 3).reshape(b, l, D_RET).astype(x.dtype)
    y_a = (ret * jax.nn.silu(g_ret)) @ w_ret_out

    zc = c_gate * z
    zp = jnp.pad(zc, ((0, 0), (CONV_WIDTH - 1, 0), (0, 0)))
    conv = sum(conv_w[j] * zp[:, j:j + l, :] for j in range(CONV_WIDTH))
    y_b = (b_gate * conv) @ w_conv_out

    merged = jax.nn.sigmoid(gate_a) * y_a + jax.nn.sigmoid(gate_b) * y_b
    h = x + merged @ w_out

    u2 = rmsnorm(h, g_mlp)
    return h + jnp.square(jax.nn.relu(u2 @ w_up)) @ w_down


def setup_inputs(seed: int = 0) -> dict:
    key = jax.random.key(seed)
    ks = jax.random.split(key, 13)
    f32 = jnp.float32
    nrm = lambda k, shape, scale: jax.random.normal(k, shape, f32) * scale
    return {
        "x": nrm(ks[0], (BATCH, SEQ, D_MODEL), 1.0),
        "meta_tokens": nrm(ks[1], (N_META, D_MODEL), 1.0),
        "norm_mix_g": 1.0 + nrm(ks[2], (DEPTH, D_MODEL), 0.02),
        "w_in": nrm(ks[3], (DEPTH, D_MODEL, D_IN), D_MODEL ** -0.5),
        "conv_w": nrm(ks[4], (DEPTH, CONV_WIDTH, D_CONV), CONV_WIDTH ** -0.5),
        "w_ret_out": nrm(ks[5], (DEPTH, D_RET, D_MODEL), D_RET ** -0.5),
        "w_conv_out": nrm(ks[6], (DEPTH, D_CONV, D_MODEL), D_CONV ** -0.5),
        "w_out": nrm(ks[7], (DEPTH, D_MODEL, D_MODEL), D_MODEL ** -0.5),
        "norm_mlp_g": 1.0 + nrm(ks[8], (DEPTH, D_MODEL), 0.02),
        "w_up": nrm(ks[9], (DEPTH, D_MODEL, D_FF), D_MODEL ** -0.5),
        "w_down": nrm(ks[10], (DEPTH, D_FF, D_MODEL), D_FF ** -0.5),
        "final_norm_g": 1.0 + nrm(ks[11], (D_MODEL,), 0.02),
    }


def reference(x, meta_tokens, norm_mix_g, w_in, conv_w, w_ret_out, w_conv_out, w_out,
              norm_mlp_g, w_up, w_down, final_norm_g):
    b = x.shape[0]
    meta = jnp.broadcast_to(meta_tokens.astype(x.dtype)[None], (b, N_META, D_MODEL))
    h = jnp.concatenate([meta, x], axis=1)
    pos = jnp.arange(h.shape[1], dtype=jnp.float32)
    for i in range(DEPTH):
        h = hybrid_layer(h, pos, norm_mix_g[i], w_in[i], conv_w[i], w_ret_out[i], w_conv_out[i],
                         w_out[i], norm_mlp_g[i], w_up[i], w_down[i])
    h = rmsnorm(h, final_norm_g)
    return h[:, N_META:]
```

```python
import contextlib
import numpy as np
import concourse.bass as bass
import concourse.mybir as mybir
from concourse.bass_utils import run_bass_kernel_spmd

F32 = mybir.dt.float32
BF16 = mybir.dt.bfloat16
U8 = mybir.dt.uint8
ALU = mybir.AluOpType
AF = mybir.ActivationFunctionType
ENGS = ["sync", "scalar", "vector", "gpsimd", "tensor"]
EPS = 1e-6
ROPE_BASE = 10000.0
NW = 3
P2SUB = 9
PASS2_SPLIT = 0


class Prog:
    def __init__(self, nc, es):
        self.nc = nc
        self.es = es
        self.ops = {e: [] for e in ENGS}
        self.sems = {}
        self.esem = {e: self._newsem("p_" + e) for e in ENGS}
        self.ecount = {e: 0 for e in ENGS}
        self.waited = {e: {} for e in ENGS}
        self.dsem = {}
        self.lastw = {}
        self.readers = {}
        self.nops = 0

    def _newsem(self, name):
        s = self.es.enter_context(self.nc.semaphore(name))
        self.sems[id(s)] = s
        return s

    def op(self, eng, fn, reads=(), writes=(), dma=None, ndma=1):
        deps = {}
        for k in reads:
            t = self.lastw.get(k)
            if t is not None:
                deps[t[0]] = max(deps.get(t[0], 0), t[1])
        for k in writes:
            t = self.lastw.get(k)
            if t is not None:
                deps[t[0]] = max(deps.get(t[0], 0), t[1])
            for sid, v in self.readers.get(k, {}).items():
                deps[sid] = max(deps.get(sid, 0), v)
        waits = []
        wd = self.waited[eng]
        own = id(self.esem[eng])
        for sid, v in deps.items():
            if sid == own and eng == "tensor":
                continue
            if wd.get(sid, 0) >= v:
                continue
            wd[sid] = v
            waits.append((self.sems[sid], v))
        if dma is not None:
            if dma not in self.dsem:
                self.dsem[dma] = [self._newsem("d_" + dma), 0]
            ent = self.dsem[dma]
            ent[1] += 16 * ndma
            tok = (id(ent[0]), ent[1])
            inc = (ent[0], 16)
        else:
            self.ecount[eng] += 1
            tok = (own, self.ecount[eng])
            inc = (self.esem[eng], 1)
        self.ops[eng].append((waits, fn, inc, dma is not None))
        for k in writes:
            self.lastw[k] = tok
            self.readers[k] = {}
        for k in reads:
            r = self.readers.setdefault(k, {})
            r[tok[0]] = max(r.get(tok[0], 0), tok[1])
        self.nops += 1
        return tok

    def alias(self, new_keys, old_keys):
        merged = {}
        for k in old_keys:
            t = self.lastw.get(k)
            if t is not None:
                merged[t[0]] = max(merged.get(t[0], 0), t[1])
            for sid, v in self.readers.get(k, {}).items():
                merged[sid] = max(merged.get(sid, 0), v)
        for k in new_keys:
            self.lastw[k] = None
            self.readers[k] = dict(merged)

    def wait_all(self, eng, toks):
        waits = []
        wd = self.waited[eng]
        for sid, v in toks:
            if wd.get(sid, 0) >= v:
                continue
            wd[sid] = v
            waits.append((self.sems[sid], v))
        self.ops[eng].append((waits, None, None, False))

    def emit(self):
        with self.nc.Block() as block:
            def mk(engname):
                def body(e):
                    for waits, fn, inc, isdma in self.ops[engname]:
                        for sem, v in waits:
                            e.wait_ge(sem, v)
                        if fn is None:
                            continue
                        r = fn(e)
                        if isdma:
                            for ins in r:
                                ins.then_inc(inc[0], inc[1])
                        else:
                            r.then_inc(inc[0], inc[1])
                return body
            block.sync(mk("sync"))
            block.scalar(mk("scalar"))
            block.vector(mk("vector"))
            block.gpsimd(mk("gpsimd"))
            block.tensor(mk("tensor"))


class Cfg:
    def __init__(self, D, H, DFF, NCH, depth, T0=16):
        self.D, self.H, self.DFF, self.NCH, self.depth, self.T0 = D, H, DFF, NCH, depth, T0
        self.DH = 256
        self.DR = H * 256
        self.DC = self.DR
        assert D == 2 * self.DR
        self.KC = D // 128
        self.RC = self.DR // 128
        self.CC = self.DC // 128
        self.FC = DFF // 128
        self.NQ = DFF // D
        self.T = T0 + 128 * NCH
        assert NCH % 4 == 0
        ng0 = -(-self.T // 512)
        base = -(-(-(-self.T // ng0)) // 16) * 16
        self.groups = [(i * base, min(base, self.T - i * base)) for i in range(ng0)]
        ng = len(self.groups)
        self.sets = [list(range(i, min(i + 3, ng))) for i in range(0, ng, 3)]
        self.chunks = [(0, T0)] + [(T0 + 128 * i, 128) for i in range(NCH)]
        self.NIN = 4 * self.RC + 3 * self.CC + 2 * self.KC
        self.NBLK = self.NIN + self.KC + self.KC + self.FC + self.NQ * self.KC
        order = []
        for h in range(H):
            for nm in ("q", "k", "v", "g"):
                for j in range(2):
                    order.append((nm, h, j))
        for i in range(self.CC):
            order += [("c", i, 0), ("z", i, 0), ("b", i, 0)]
        for m in range(self.KC):
            order.append(("ga", m, 0))
        for m in range(self.KC):
            order.append(("gb", m, 0))
        self.in_order = order

    def in_src_block(self, ent):
        nm, a, j = ent
        RC, CC, KC = self.RC, self.CC, self.KC
        if nm == "q":
            return 2 * a + j
        if nm == "k":
            return RC + 2 * a + j
        if nm == "v":
            return 2 * RC + 2 * a + j
        if nm == "g":
            return 3 * RC + 2 * a + j
        if nm == "c":
            return 4 * RC + a
        if nm == "b":
            return 4 * RC + CC + a
        if nm == "z":
            return 4 * RC + 2 * CC + a
        if nm == "ga":
            return 4 * RC + 3 * CC + a
        if nm == "gb":
            return 4 * RC + 3 * CC + KC + a
        raise ValueError(nm)


FULL = dict(D=4096, H=8, DFF=16384, NCH=16, depth=2)


def blockify(W):
    K, M = W.shape
    return np.ascontiguousarray(
        W.reshape(K // 128, 128, M // 128, 128).transpose(2, 1, 0, 3)).reshape(M // 128, 128, (K // 128) * 128)


def prep_weights(cfg, l, w_in, w_ret_out, w_conv_out, w_out, w_up, w_down):
    out = np.empty((cfg.NBLK, 128, cfg.KC * 128), np.float32)
    wb = blockify(np.asarray(w_in[l]))
    idx = [cfg.in_src_block(e) for e in cfg.in_order]
    n = 0
    out[n:n + cfg.NIN] = wb[idx]
    n += cfg.NIN
    del wb
    out[n:n + cfg.KC] = blockify(np.concatenate([np.asarray(w_ret_out[l]), np.asarray(w_conv_out[l])], axis=0))
    n += cfg.KC
    out[n:n + cfg.KC] = blockify(np.asarray(w_out[l]))
    n += cfg.KC
    out[n:n + cfg.FC] = blockify(np.asarray(w_up[l]))
    n += cfg.FC
    wd = np.asarray(w_down[l])
    for qd in range(cfg.NQ):
        out[n:n + cfg.KC] = blockify(wd[qd * cfg.D:(qd + 1) * cfg.D])
        n += cfg.KC
    assert n == cfg.NBLK
    return out


def const_tables(cfg):
    T, H = cfg.T, cfg.H
    d = 256
    inv_freq = (ROPE_BASE ** (-np.arange(0, d, 2, dtype=np.float32) / d)).astype(np.float32)
    pos = np.arange(T, dtype=np.float32)
    ang = (pos[None, :] * inv_freq[:, None]).astype(np.float32)
    cs = np.stack([np.cos(ang), np.sin(ang)]).astype(np.float32)
    gam = 1.0 - np.exp2(-5.0 - np.arange(H, dtype=np.float64))
    lg = np.log(gam)
    scale = 256.0 ** -0.5
    idx = np.arange(128, dtype=np.float64)
    rel = idx[None, :] - idx[:, None]
    maskT = np.where(rel[None] >= 0, np.exp(lg[:, None, None] * np.maximum(rel, 0.0)[None]), 0.0) * scale
    j = (np.arange(T) + (128 - cfg.T0)) % 128
    qd = np.exp(lg[:, None] * (j[None, :] + 1.0))
    QD = np.broadcast_to(qd[:, None, :], (H, 128, T))
    KD = np.zeros((H, 128, 2))
    p = np.arange(128)
    KD[:, :, 1] = np.exp(lg[:, None] * (127.0 - p[None, :])) * scale
    KD[:, :cfg.T0, 0] = np.exp(lg[:, None] * (cfg.T0 - 1.0 - p[None, :cfg.T0])) * scale
    cd = np.exp(lg * 128.0)
    return dict(cs=cs, maskT=np.ascontiguousarray(maskT.astype(np.float32).transpose(1, 0, 2)),
                QD=np.ascontiguousarray(QD.astype(np.float32)),
                KD=np.ascontiguousarray(KD.astype(np.float32).transpose(1, 0, 2)),
                cd=[float(np.float32(c)) for c in cd],
                ident=np.eye(128, dtype=np.float32))


def prep_small(cfg, norm_mix_g, norm_mlp_g, final_norm_g, conv_w):
    KC, CC, L = cfg.KC, cfg.CC, cfg.depth
    gains = np.zeros((128, 2 * L + 1, KC), np.float32)
    for l in range(L):
        gains[:, 2 * l, :] = np.asarray(norm_mix_g[l]).reshape(KC, 128).T
        gains[:, 2 * l + 1, :] = np.asarray(norm_mlp_g[l]).reshape(KC, 128).T
    gains[:, 2 * L, :] = np.asarray(final_norm_g).reshape(KC, 128).T
    cw = np.zeros((128, L, CC, 3), np.float32)
    for l in range(L):
        cw[:, l] = np.asarray(conv_w[l]).reshape(3, CC, 128).transpose(2, 1, 0)
    return gains, cw


def build(cfg, limit=None):
    D, H, KC, RC, CC, FC, NQ, T, T0, NCH, L = (cfg.D, cfg.H, cfg.KC, cfg.RC, cfg.CC, cfg.FC, cfg.NQ,
                                               cfg.T, cfg.T0, cfg.NCH, cfg.depth)
    SEQ = 128 * NCH
    GROUPS, SETS, CHUNKS = cfg.groups, cfg.sets, cfg.chunks
    NG = len(GROUPS)
    consts = const_tables(cfg)
    CD = consts["cd"]

    nc = bass.Bass("TRN2", target_bir_lowering=False)
    dt_in = lambda name, shape: nc.dram_tensor(name, shape, F32, kind="ExternalInput").ap()
    x_in = dt_in("x", [SEQ, D])
    meta_in = dt_in("meta", [T0, D])
    W_in = [dt_in(f"w{l}", [cfg.NBLK, 128, KC * 128]) for l in range(L)]
    gains_in = dt_in("gains", [128, (2 * L + 1) * KC])
    cw_in = dt_in("cw", [128, L * CC * 3])
    cs_in = dt_in("cs", [2, 128, T])
    mask_in = dt_in("maskT", [128, H * 128])
    qd_in = dt_in("QD", [H, 128, T])
    kd_in = dt_in("KD", [128, H * 2])
    ident_in = dt_in("ident", [128, 128])
    y_out = nc.dram_tensor("y", [SEQ, D], F32, kind="ExternalOutput").ap()

    scr = lambda name, shape, dt: nc.dram_tensor(name, shape, dt, kind="Internal").ap()
    R = scr("R", [KC, 128, T], F32)
    QKV = scr("QKV", [H, 6, 128, T], F32)
    SG = scr("SG", [RC, 128, T], F32)
    YB = scr("YB", [CC, 128, T], BF16)
    GA = scr("GA", [KC, 128, T], F32)
    GB = scr("GB", [KC, 128, T], F32)
    GT = scr("GT", [RC, 128, T], BF16)
    MG = scr("MG", [KC, 128, T], BF16)
    HID = scr("HID", [FC, 128, T], BF16)

    es = contextlib.ExitStack()
    with es:
        P = Prog(nc, es)
        ARENA = 211000
        arena = es.enter_context(nc.sbuf_tensor("arena", [128, ARENA], U8)).ap()

        def view(off, shape, dt):
            nb = int(np.prod(shape[1:])) * (4 if dt == F32 else 2)
            assert off % 32 == 0, off
            assert off + nb <= ARENA, (off, nb)
            v = arena[:, off:off + nb].bitcast(dt)
            if len(shape) == 3:
                v = v.rearrange("p (a b) -> p a b", a=shape[1])
            return v

        al = lambda n: (n + 63) // 64 * 64
        ROW4 = al(T * 4)
        ROW2 = al(T * 2)
        WSZ = KC * 128 * 2
        W_OFF = 0
        wbf = [view(W_OFF + i * WSZ, [128, KC, 128], BF16) for i in range(NW)]
        C_OFF = W_OFF + NW * WSZ
        co = [C_OFF]

        def calloc(shape, dt):
            nb = al(int(np.prod(shape[1:])) * (4 if dt == F32 else 2))
            v = view(co[0], shape, dt)
            co[0] += nb
            return v
        ident = calloc([128, 128], F32)
        onesD = calloc([128, 128], F32)
        ones256 = calloc([128, 128], F32)
        gains = calloc([128, (2 * L + 1), KC], F32)
        cwt = calloc([128, L * CC * 3], F32)
        kdt = calloc([128, H * 2], F32)
        maskT = calloc([128, H, 128], F32)
        A_OFF = al(co[0])
        A_SZ = al(KC * T * 2)
        Abuf = view(A_OFF, [128, KC, T], BF16)
        X_OFF = A_OFF + A_SZ
        X_SZ = ARENA - X_OFF
        assert X_SZ >= 5 * ROW4 + 64, X_SZ

        ps = [es.enter_context(nc.psum_tensor(f"ps{i}", [128, 512], F32)) for i in range(8)]
        bankc = [0]

        def banks(n):
            r = [(bankc[0] + i) % 8 for i in range(n)]
            bankc[0] = (bankc[0] + n) % 8
            return r

        region_keys = {"A": [], "X": []}

        def enter_phase(region, keys):
            P.alias(keys, region_keys[region])
            region_keys[region] = list(keys)

        def enter_both(keys):
            P.alias(keys, region_keys["A"] + region_keys["X"])
            region_keys["A"] = list(keys)
            region_keys["X"] = list(keys)

        AKEYS = [("A", kc) for kc in range(KC)]
        enter_phase("A", AKEYS)

        def ld(dst, src, key):
            P.op("sync", lambda e: [e.dma_start(out=dst, in_=src)], writes=[key], dma="c_" + str(key))
        ld(ident, ident_in, "ident")
        ld(gains.rearrange("p a b -> p (a b)"), gains_in, "gains")
        ld(cwt, cw_in, "cw")
        ld(kdt, kd_in, "kdt")
        ld(maskT.rearrange("p a b -> p (a b)"), mask_in, "maskT")
        P.op("vector", lambda e: e.memset(onesD, 1.0 / D), writes=["onesD"])
        P.op("vector", lambda e: e.memset(ones256, 1.0 / 256), writes=["ones256"])

        NTOT = L * cfg.NBLK
        wstate = {"next_load": 0, "next_use": 0}

        def w_load():
            j = wstate["next_load"]
            if j >= NTOT:
                return
            wstate["next_load"] += 1
            l, b = divmod(j, cfg.NBLK)
            dst = wbf[j % NW].rearrange("p a b -> p (a b)")
            src = W_in[l][b]
            P.op("gpsimd", lambda e: [e.dma_start(out=dst, in_=src)], writes=[("w", j % NW)], dma=f"w{j % NW}")

        def w_next():
            j = wstate["next_use"]
            wstate["next_use"] += 1
            while wstate["next_load"] < min(NTOT, j + NW):
                w_load()
            return wbf[j % NW], ("w", j % NW)

        APARTS = 4
        assert KC % APARTS == 0
        APSZ = KC // APARTS

        def a_load(srcs, rkeys):
            for p in range(APARTS):
                kcs = list(range(p * APSZ, (p + 1) * APSZ))
                P.op("gpsimd", lambda e, kcs=kcs: [e.dma_start(out=Abuf[:, kc, :], in_=srcs[kc]) for kc in kcs],
                     reads=[rkeys[kc] for kc in kcs], writes=[("A", kc) for kc in kcs], dma=f"Aload{p}", ndma=len(kcs))

        def lin_set(wb, wkey, st, accs, parts):
            for kcs, bk in accs:
                n_kc = len(kcs)
                psz = -(-n_kc // parts)
                for p0 in range(0, n_kc, psz):
                    sub = kcs[p0:p0 + psz]

                    def mm(e, sub=sub, p0=p0, bk=bk, n_kc=n_kc):
                        last = None
                        for i, kc in enumerate(sub):
                            for g in st:
                                off, n = GROUPS[g]
                                last = e.matmul(ps[bk[g]][:, :n], lhsT=wb[:, kc, :], rhs=Abuf[:, kc, off:off + n],
                                                start=(p0 + i == 0), stop=(p0 + i == n_kc - 1))
                        return last
                    P.op("tensor", mm, reads=[wkey] + [("A", kc) for kc in sub], writes=[("ps", bk[g]) for g in st])

        def lin_block(epi, kc_split=None, edge=False):
            wb, wkey = w_next()
            kls = [list(range(KC))] if kc_split is None else kc_split
            parts = (APARTS // len(kls)) if edge else 1
            sets = [list(range(NG))] if (edge and len(kls) == 1 and NG <= 6) else SETS
            for st in sets:
                accs = []
                for kcs in kls:
                    b = banks(len(st))
                    accs.append((kcs, {g: b[i] for i, g in enumerate(st)}))
                lin_set(wb, wkey, st, accs, parts)
                for g in st:
                    off, n = GROUPS[g]
                    epi(g, off, n, [bk[g] for _, bk in accs])

        evc = [0]

        def evac_copy(dst, bank, n, rkeys, wkeys, eng=None):
            if eng is None:
                eng = "scalar" if evc[0] % 2 == 0 else "vector"
                evc[0] += 1
            src = ps[bank][:, :n]
            if eng == "scalar":
                P.op("scalar", lambda e: e.activation(out=dst, in_=src, func=AF.Copy), reads=rkeys, writes=wkeys)
            else:
                P.op("vector", lambda e: e.tensor_copy(out=dst, in_=src), reads=rkeys, writes=wkeys)

        def act_evac(dst, bank, n, func, rkeys, wkeys):
            src = ps[bank][:, :n]
            P.op("scalar", lambda e: e.activation(out=dst, in_=src, func=func), reads=rkeys, writes=wkeys)

        def store(dst_dram, src, rkeys, dkey, chan):
            return P.op("sync", lambda e: [e.dma_start(out=dst_dram, in_=src)], reads=rkeys, writes=[dkey], dma=chan)

        def load(dst, src_dram, dkey, wkeys, chan, eng="sync"):
            return P.op(eng, lambda e: [e.dma_start(out=dst, in_=src_dram)], reads=[dkey], writes=wkeys, dma=chan)

        GK = lambda name: [(name, g) for g in range(NG)]

        def phase_T0():
            QT = 4
            XT = [view(A_OFF + i * al(D * 4), [128, D], F32) for i in range(2)]
            base = A_OFF + 2 * al(D * 4)
            XS = [view(base + i * al(KC * QT * 128 * 4), [128, KC, QT * 128], F32) for i in range(2)]
            assert base + 2 * al(KC * QT * 128 * 4) <= ARENA
            keys = [("xt", 0), ("xt", 1), ("xs", 0), ("xs", 1)]
            enter_both(keys)
            tiles = [[0]] + [list(range(1 + q, 1 + min(q + QT, NCH))) for q in range(0, NCH, QT)]
            li = 0
            for ti, tl in enumerate(tiles):
                xs = XS[ti % 2]
                tbase = CHUNKS[tl[0]][0]
                for ci in tl:
                    t0, c = CHUNKS[ci]
                    xt = XT[li % 2]
                    src = meta_in[:, :] if ci == 0 else x_in[(ci - 1) * 128: ci * 128, :]
                    P.op("sync", lambda e, xt=xt, src=src, c=c: [e.dma_start(out=xt[:c, :], in_=src)],
                         writes=[("xt", li % 2)], dma=f"xt{li % 2}")
                    lo = t0 - tbase
                    for k0 in range(0, KC, 4):
                        nk = min(4, KC - k0)
                        bk = banks(1)[0]

                        def tr(e, xt=xt, k0=k0, nk=nk, bk=bk, c=c):
                            last = None
                            for i in range(nk):
                                last = e.transpose(out=ps[bk][:, i * c:(i + 1) * c],
                                                   in_=xt[:c, (k0 + i) * 128:(k0 + i + 1) * 128], identity=ident[:c, :c])
                            return last
                        P.op("tensor", tr, reads=[("xt", li % 2), "ident"], writes=[("ps", bk)])
                        dst = xs[:, k0:k0 + nk, lo:lo + c]
                        src_ps = ps[bk][:, :nk * c].rearrange("p (a b) -> p a b", a=nk)
                        if (k0 // 4) % 2 == 0:
                            P.op("vector", lambda e, dst=dst, src_ps=src_ps: e.tensor_copy(out=dst, in_=src_ps),
                                 reads=[("ps", bk)], writes=[("xs", ti % 2)])
                        else:
                            P.op("scalar", lambda e, dst=dst, src_ps=src_ps: e.activation(out=dst, in_=src_ps, func=AF.Copy),
                                 reads=[("ps", bk)], writes=[("xs", ti % 2)])
                    li += 1
                ntok = sum(CHUNKS[ci][1] for ci in tl)
                dstR = R[:, :, tbase:tbase + ntok].rearrange("k p t -> p k t")
                P.op("gpsimd", lambda e, dstR=dstR, xs=xs, ntok=ntok: [e.dma_start(out=dstR, in_=xs[:, :, :ntok])],
                     reads=[("xs", ti % 2)], writes=[("R", kc) for kc in range(KC)], dma=f"xs{ti % 2}")

        RSTD_OFF = X_OFF + 4 * ROW4

        def enter_X_keep_rstd(keys):
            P.alias(keys, [k for k in region_keys["X"] if k != "rstd"])
            region_keys["X"] = list(keys) + ["rstd"]

        def stats_finish(acc):
            sb = banks(NG)

            def mm(e):
                last = None
                for g, (off, n) in enumerate(GROUPS):
                    last = e.matmul(ps[sb[g]][:, :n], lhsT=onesD, rhs=acc[:, off:off + n], start=True, stop=True)
                return last
            P.op("tensor", mm, reads=["rstd", "onesD"], writes=[("ps", b) for b in sb])
            for g, (off, n) in enumerate(GROUPS):
                P.op("scalar", lambda e, g=g, off=off, n=n: e.activation(out=acc[:, off:off + n], in_=ps[sb[g]][:, :n],
                                                                      func=AF.Sqrt, bias=EPS, scale=1.0),
                     reads=[("ps", sb[g])], writes=["rstd"])
            P.op("vector", lambda e: e.reciprocal(out=acc, in_=acc), reads=["rstd"], writes=["rstd"])

        def phase_norm(gi, have_stats=False, final=False):
            NXC = 4 if (have_stats and not final) else 2
            xc = [view(X_OFF + i * ROW4, [128, T], F32) for i in range(NXC)]
            sq = [view(X_OFF + (2 + i) * ROW4, [128, T], F32) for i in range(2)]
            rstd = view(RSTD_OFF, [128, T], F32)
            keys = [("xc", i) for i in range(NXC)] + ([("sq", 0), ("sq", 1)] if NXC == 2 else [])
            if have_stats:
                enter_X_keep_rstd(keys)
            else:
                enter_phase("X", keys + ["rstd"])
                sb = banks(NG)
                for kc in range(KC):
                    x_ = xc[kc % NXC]
                    s_ = sq[kc % 2]
                    load(x_, R[kc], ("R", kc), [("xc", kc % NXC)], f"xc{kc % NXC}")
                    P.op("scalar", lambda e, x_=x_, s_=s_: e.activation(out=s_, in_=x_, func=AF.Square),
                         reads=[("xc", kc % NXC)], writes=[("sq", kc % 2)])

                    def mm(e, s_=s_, kc=kc):
                        last = None
                        for g, (off, n) in enumerate(GROUPS):
                            last = e.matmul(ps[sb[g]][:, :n], lhsT=onesD, rhs=s_[:, off:off + n],
                                            start=(kc == 0), stop=(kc == KC - 1))
                        return last
                    P.op("tensor", mm, reads=[("sq", kc % 2), "onesD"], writes=[("ps", b) for b in sb])
                for g, (off, n) in enumerate(GROUPS):
                    P.op("scalar", lambda e, g=g, off=off, n=n: e.activation(out=rstd[:, off:off + n], in_=ps[sb[g]][:, :n],
                                                                          func=AF.Sqrt, bias=EPS, scale=1.0),
                         reads=[("ps", sb[g])], writes=["rstd"])
                P.op("vector", lambda e: e.reciprocal(out=rstd, in_=rstd), reads=["rstd"], writes=["rstd"])
            if final:
                return xc, sq, rstd
            if NXC == 2:
                rows2 = [(xc[0], ("xc", 0), "xc0"), (xc[1], ("xc", 1), "xc1"), (sq[0], ("sq", 0), "sqld0"), (sq[1], ("sq", 1), "sqld1")]
            else:
                rows2 = [(xc[i], ("xc", i), f"xc{i}") for i in range(NXC)]
            for kc in range(KC):
                x_, xk, xch = rows2[kc % len(rows2)]
                load(x_, R[kc], ("R", kc), [xk], xch)
                if PASS2_SPLIT and NXC == 4 and kc % 4 == 3:
                    P.op("scalar", lambda e, x_=x_, kc=kc: e.activation(out=x_, in_=x_, func=AF.Identity, scale=gains[:, gi, kc:kc + 1]),
                         reads=[("xc", kc % NXC), "gains"], writes=[("xc", kc % NXC)])
                    P.op("gpsimd", lambda e, x_=x_, kc=kc: e.tensor_tensor(out=Abuf[:, kc, :], in0=x_, in1=rstd, op=ALU.mult),
                         reads=[("xc", kc % NXC), "rstd"], writes=[("A", kc)])
                else:
                    P.op("vector", lambda e, x_=x_, kc=kc: e.scalar_tensor_tensor(
                        out=Abuf[:, kc, :], in0=x_, scalar=gains[:, gi, kc:kc + 1], in1=rstd, op0=ALU.mult, op1=ALU.mult),
                        reads=[xk, "rstd", "gains"], writes=[("A", kc)])

        def phase_P1(l):
            o = X_OFF
            crow = view(o, [128, T], F32); o += ROW4
            czrow = view(o, [128, T + 2], F32); o += al((T + 2) * 4)
            orow = []
            for i in range(2):
                orow.append(view(o, [128, T], F32)); o += ROW4
            brow = []
            for i in range(2):
                brow.append(view(o, [128, T], BF16)); o += ROW2
            keys = GK("crow") + GK("czrow") + ["czpad"] + GK(("orow", 0)) + GK(("orow", 1)) + GK(("brow", 0)) + GK(("brow", 1))
            enter_phase("X", keys)
            P.op("vector", lambda e: e.memset(czrow[:, 0:2], 0.0), writes=["czpad"])
            oc = [0]
            bc = [0]
            for ent in cfg.in_order:
                nm, a, j = ent
                if nm in ("q", "k", "v", "g", "ga", "gb"):
                    oi = oc[0] % 2
                    oc[0] += 1
                    orw = orow[oi]
                    if nm in ("q", "k", "v"):
                        dst = QKV[a, {"q": 0, "k": 2, "v": 4}[nm] + j]
                        dkey = ("QKV", a, {"q": 0, "k": 2, "v": 4}[nm] + j)
                        func = None
                    elif nm == "g":
                        dst = SG[2 * a + j]
                        dkey = ("SG", 2 * a + j)
                        func = AF.Silu
                    elif nm == "ga":
                        dst = GA[a]
                        dkey = ("GA", a)
                        func = AF.Sigmoid
                    else:
                        dst = GB[a]
                        dkey = ("GB", a)
                        func = AF.Sigmoid

                    def epi(g, off, n, bks, orw=orw, oi=oi, func=func):
                        if func is None:
                            evac_copy(orw[:, off:off + n], bks[0], n, [("ps", bks[0])], [(("orow", oi), g)])
                        else:
                            act_evac(orw[:, off:off + n], bks[0], n, func, [("ps", bks[0])], [(("orow", oi), g)])
                    lin_block(epi)
                    store(dst, orw, GK(("orow", oi)), dkey, f"orow{oi}")
                elif nm == "c":
                    def epi(g, off, n, bks):
                        evac_copy(crow[:, off:off + n], bks[0], n, [("ps", bks[0])], [("crow", g)])
                    lin_block(epi)
                elif nm == "z":
                    def epi(g, off, n, bks):
                        src = ps[bks[0]][:, :n]
                        P.op("vector", lambda e: e.tensor_tensor(out=czrow[:, 2 + off:2 + off + n], in0=src,
                                                                 in1=crow[:, off:off + n], op=ALU.mult),
                             reads=[("ps", bks[0]), ("crow", g)], writes=[("czrow", g)])
                    lin_block(epi)
                    cwb = (l * CC + a) * 3
                    P.op("vector", lambda e, cwb=cwb: e.tensor_scalar(out=crow, in0=czrow[:, 2:2 + T], scalar1=cwt[:, cwb + 2:cwb + 3],
                                                                      scalar2=None, op0=ALU.mult),
                         reads=GK("czrow") + ["cw"], writes=GK("crow"))
                    P.op("vector", lambda e, cwb=cwb: e.scalar_tensor_tensor(out=crow, in0=czrow[:, 1:1 + T], scalar=cwt[:, cwb + 1:cwb + 2],
                                                                             in1=crow, op0=ALU.mult, op1=ALU.add),
                         reads=GK("czrow") + ["czpad", "cw"] + GK("crow"), writes=GK("crow"))
                    P.op("vector", lambda e, cwb=cwb: e.scalar_tensor_tensor(out=crow, in0=czrow[:, 0:T], scalar=cwt[:, cwb:cwb + 1],
                                                                             in1=crow, op0=ALU.mult, op1=ALU.add),
                         reads=GK("czrow") + ["czpad", "cw"] + GK("crow"), writes=GK("crow"))
                else:
                    bi = bc[0] % 2
                    bc[0] += 1
                    brw = brow[bi]

                    def epi(g, off, n, bks, brw=brw, bi=bi):
                        src = ps[bks[0]][:, :n]
                        P.op("vector", lambda e: e.tensor_tensor(out=brw[:, off:off + n], in0=src,
                                                                 in1=crow[:, off:off + n], op=ALU.mult),
                             reads=[("ps", bks[0]), ("crow", g)], writes=[(("brow", bi), g)])
                    lin_block(epi)
                    store(YB[a], brw, GK(("brow", bi)), ("YB", a), f"brow{bi}")

        def phase_P2():
            o = [A_OFF]

            def ralloc(shape, dt):
                nb = al(int(np.prod(shape[1:])) * (4 if dt == F32 else 2))
                v = view(o[0], shape, dt)
                o[0] += nb
                return v
            cosT = ralloc([128, T], F32)
            sinT = ralloc([128, T], F32)
            qkv = [ralloc([128, T], F32) for _ in range(6)]
            taq = ralloc([128, T], F32)
            tak = ralloc([128, T], F32)
            tb = ralloc([128, T], F32)
            qr = [ralloc([128, T], BF16) for _ in range(2)]
            qd = [ralloc([128, T], BF16) for _ in range(2)]
            kr = [ralloc([128, T], BF16) for _ in range(2)]
            kd = ralloc([128, NCH + 1, 256], BF16)
            vtm = ralloc([128, NCH + 1, 256], BF16)
            sg = [ralloc([128, T], F32) for _ in range(2)]
            QDr = ralloc([128, T], F32)
            gout = [ralloc([128, T], BF16) for _ in range(2)]
            NB3 = 3
            NB2 = 2
            STm = [ralloc([128, 128], BF16) for _ in range(NB3)]
            S = [ralloc([128, 256], F32) for _ in range(2)]
            Sbf = [[ralloc([128, 256], BF16) for _ in range(2)] for _ in range(NB3)]
            sqt = [ralloc([128, 256], F32) for _ in range(NB2)]
            rst = [ralloc([128, 128], F32) for _ in range(NB2)]
            rs = [ralloc([128, 256], F32) for _ in range(NB2)]
            assert o[0] <= ARENA, o[0]
            keys = (["cos", "sin", "taq", "tak", "tb", "QDr", "kd", "vtm"] + [("qkv", i) for i in range(6)]
                    + [("qr", i) for i in range(2)] + [("qd", i) for i in range(2)] + [("kr", i) for i in range(2)]
                    + [("sg", i) for i in range(2)] + [("gout", i) for i in range(2)]
                    + [("STm", i) for i in range(NB3)] + [("S", i) for i in range(2)]
                    + [("Sbf", i, j) for i in range(NB3) for j in range(2)]
                    + [("sqt", i) for i in range(NB2)] + [("rst", i) for i in range(NB2)] + [("rs", i) for i in range(NB2)])
            enter_both(keys)
            P.op("sync", lambda e: [e.dma_start(out=cosT, in_=cs_in[0])], writes=["cos"], dma="cos")
            P.op("sync", lambda e: [e.dma_start(out=sinT, in_=cs_in[1])], writes=["sin"], dma="sin")
            TT = lambda out, a, b, op: (lambda e: e.tensor_tensor(out=out, in0=a, in1=b, op=op))
            NCK = len(CHUNKS)

            def head_loads(h):
                for i in range(6):
                    load(qkv[i], QKV[h, i], ("QKV", h, i), [("qkv", i)], f"qkv{i}")
                P.op("sync", lambda e, h=h: [e.dma_start(out=QDr, in_=qd_in[h])], writes=["QDr"], dma="QDr")

            def sg_loads(h):
                for j in range(2):
                    load(sg[j], SG[2 * h + j], ("SG", 2 * h + j), [("sg", j)], f"sg{j}")

            def rot(x0, x1, k0, k1, ta, tak_):
                P.op("vector", TT(ta, x0, cosT, ALU.mult), reads=[k0, "cos"], writes=[tak_])
                P.op("vector", TT(tb, x1, sinT, ALU.mult), reads=[k1, "sin"], writes=["tb"])
                P.op("vector", TT(ta, ta, tb, ALU.subtract), reads=[tak_, "tb"], writes=[tak_])
                P.op("vector", TT(tb, x0, sinT, ALU.mult), reads=[k0, "sin"], writes=["tb"])
                P.op("vector", TT(x1, x1, cosT, ALU.mult), reads=[k1, "cos"], writes=[k1])
                P.op("vector", TT(x1, x1, tb, ALU.add), reads=[k1, "tb"], writes=[k1])

            head_loads(0)
            sg_loads(0)
            for h in range(H):
                for ci, (t0, c) in enumerate(CHUNKS):
                    bk = banks(1)[0]

                    def trv(e, t0=t0, c=c, bk=bk):
                        e.transpose(out=ps[bk][:c, 0:128], in_=qkv[4][:, t0:t0 + c], identity=ident)
                        return e.transpose(out=ps[bk][:c, 128:256], in_=qkv[5][:, t0:t0 + c], identity=ident)
                    P.op("tensor", trv, reads=[("qkv", 4), ("qkv", 5), "ident"], writes=[("ps", bk)])
                    P.op("scalar", lambda e, ci=ci, c=c, bk=bk: e.activation(out=vtm[:c, ci, :], in_=ps[bk][:c, 0:256], func=AF.Copy),
                         reads=[("ps", bk)], writes=["vtm"])
                rot(qkv[2], qkv[3], ("qkv", 2), ("qkv", 3), tak, "tak")
                P.op("scalar", lambda e: e.activation(out=kr[0], in_=tak, func=AF.Copy), reads=["tak"], writes=[("kr", 0)])
                P.op("scalar", lambda e: e.activation(out=kr[1], in_=qkv[3], func=AF.Copy), reads=[("qkv", 3)], writes=[("kr", 1)])
                for ci, (t0, c) in enumerate(CHUNKS):
                    bk = banks(1)[0]

                    def trk(e, t0=t0, c=c, bk=bk):
                        e.transpose(out=ps[bk][:c, 0:128], in_=tak[:, t0:t0 + c], identity=ident)
                        return e.transpose(out=ps[bk][:c, 128:256], in_=qkv[3][:, t0:t0 + c], identity=ident)
                    P.op("tensor", trk, reads=["tak", ("qkv", 3), "ident"], writes=[("ps", bk)])
                    col = h * 2 + (0 if ci == 0 else 1)
                    P.op("vector", lambda e, ci=ci, c=c, bk=bk, col=col: e.tensor_scalar(
                        out=kd[:c, ci, :], in0=ps[bk][:c, 0:256], scalar1=kdt[:c, col:col + 1], scalar2=None, op0=ALU.mult),
                        reads=[("ps", bk), "kdt"], writes=["kd"])
                rot(qkv[0], qkv[1], ("qkv", 0), ("qkv", 1), taq, "taq")
                P.op("scalar", lambda e: e.activation(out=qr[0], in_=taq, func=AF.Copy), reads=["taq"], writes=[("qr", 0)])
                P.op("scalar", lambda e: e.activation(out=qr[1], in_=qkv[1], func=AF.Copy), reads=[("qkv", 1)], writes=[("qr", 1)])
                P.op("vector", TT(qd[0], taq, QDr, ALU.mult), reads=["taq", "QDr"], writes=[("qd", 0)])
                P.op("vector", TT(qd[1], qkv[1], QDr, ALU.mult), reads=[("qkv", 1), "QDr"], writes=[("qd", 1)])
                if h + 1 < H:
                    head_loads(h + 1)

                def emit_ST(ci, h=h):
                    t0, c = CHUNKS[ci]
                    bk = banks(1)[0]
                    sl = ci % NB3

                    def mm(e):
                        e.matmul(ps[bk][:c, :c], lhsT=kr[0][:, t0:t0 + c], rhs=qr[0][:, t0:t0 + c], start=True, stop=False)
                        return e.matmul(ps[bk][:c, :c], lhsT=kr[1][:, t0:t0 + c], rhs=qr[1][:, t0:t0 + c], start=False, stop=True)
                    P.op("tensor", mm, reads=[("kr", 0), ("kr", 1), ("qr", 0), ("qr", 1)], writes=[("ps", bk)])
                    P.op("vector", lambda e: e.tensor_tensor(out=STm[sl][:c, :c], in0=ps[bk][:c, :c], in1=maskT[:c, h, :c], op=ALU.mult),
                         reads=[("ps", bk), "maskT"], writes=[("STm", sl)])

                def emit_state(ci, h=h):
                    t0, c = CHUNKS[ci]
                    bk = banks(1)[0]
                    nsl = (ci + 1) % NB3

                    def mm(e):
                        e.matmul(ps[bk][:, 0:256], lhsT=kd[:c, ci, 0:128], rhs=vtm[:c, ci, :], start=True, stop=True)
                        return e.matmul(ps[bk][:, 256:512], lhsT=kd[:c, ci, 128:256], rhs=vtm[:c, ci, :], start=True, stop=True)
                    P.op("tensor", mm, reads=["kd", "vtm"], writes=[("ps", bk)])
                    for dh in range(2):
                        src = ps[bk][:, dh * 256:(dh + 1) * 256]
                        if ci == 0:
                            P.op("vector", lambda e, dh=dh, src=src: e.tensor_copy(out=S[dh], in_=src),
                                 reads=[("ps", bk)], writes=[("S", dh)])
                        else:
                            P.op("vector", lambda e, dh=dh, src=src: e.scalar_tensor_tensor(
                                out=S[dh], in0=S[dh], scalar=CD[h], in1=src, op0=ALU.mult, op1=ALU.add),
                                reads=[("ps", bk), ("S", dh)], writes=[("S", dh)])
                        P.op("scalar", lambda e, dh=dh: e.activation(out=Sbf[nsl][dh], in_=S[dh], func=AF.Copy),
                             reads=[("S", dh)], writes=[("Sbf", nsl, dh)])

                def emit_retA(ci):
                    t0, c = CHUNKS[ci]
                    bk = banks(1)[0]
                    sl3 = ci % NB3
                    sl = ci % NB2

                    def mm(e):
                        last = None
                        for dvh in range(2):
                            o_ = ps[bk][:, dvh * c:(dvh + 1) * c]
                            last = e.matmul(o_, lhsT=vtm[:c, ci, dvh * 128:(dvh + 1) * 128], rhs=STm[sl3][:c, :c],
                                            start=True, stop=(ci == 0))
                            if ci > 0:
                                for dh in range(2):
                                    last = e.matmul(o_, lhsT=Sbf[sl3][dh][:, dvh * 128:(dvh + 1) * 128], rhs=qd[dh][:, t0:t0 + c],
                                                    start=False, stop=(dh == 1))
                        return last
                    rd = ["vtm", ("STm", sl3)] + ([("Sbf", sl3, 0), ("Sbf", sl3, 1), ("qd", 0), ("qd", 1)] if ci > 0 else [])
                    P.op("tensor", mm, reads=rd, writes=[("ps", bk)])
                    P.op("scalar", lambda e: e.activation(out=sqt[sl][:, :2 * c], in_=ps[bk][:, :2 * c], func=AF.Square),
                         reads=[("ps", bk)], writes=[("sqt", sl)])
                    return bk

                def emit_retB(ci, bk):
                    t0, c = CHUNKS[ci]
                    sl = ci % NB2
                    bn = banks(1)[0]

                    def mm2(e):
                        e.matmul(ps[bn][:, :c], lhsT=ones256, rhs=sqt[sl][:, 0:c], start=True, stop=False)
                        return e.matmul(ps[bn][:, :c], lhsT=ones256, rhs=sqt[sl][:, c:2 * c], start=False, stop=True)
                    P.op("tensor", mm2, reads=[("sqt", sl), "ones256"], writes=[("ps", bn)])
                    P.op("scalar", lambda e: e.activation(out=rst[sl][:, :c], in_=ps[bn][:, :c], func=AF.Sqrt, bias=EPS, scale=1.0),
                         reads=[("ps", bn)], writes=[("rst", sl)])
                    P.op("vector", lambda e: e.reciprocal(out=rst[sl][:, :c], in_=rst[sl][:, :c]), reads=[("rst", sl)], writes=[("rst", sl)])
                    for dvh in range(2):
                        P.op("vector", lambda e, dvh=dvh: e.tensor_tensor(out=rs[sl][:, dvh * c:(dvh + 1) * c], in0=rst[sl][:, :c],
                                                                          in1=sg[dvh][:, t0:t0 + c], op=ALU.mult),
                             reads=[("rst", sl), ("sg", dvh)], writes=[("rs", sl)])
                    for dvh in range(2):
                        P.op("vector", lambda e, dvh=dvh: e.tensor_tensor(out=gout[dvh][:, t0:t0 + c], in0=ps[bk][:, dvh * c:(dvh + 1) * c],
                                                                          in1=rs[sl][:, dvh * c:(dvh + 1) * c], op=ALU.mult),
                             reads=[("ps", bk), ("rs", sl)], writes=[("gout", dvh)])

                emit_ST(0)
                if NCK > 1:
                    emit_ST(1)
                    emit_state(0)
                for ci in range(NCK):
                    bk = emit_retA(ci)
                    if ci + 2 < NCK:
                        emit_ST(ci + 2)
                    if ci + 2 < NCK:
                        emit_state(ci + 1)
                    emit_retB(ci, bk)
                for dvh in range(2):
                    store(GT[2 * h + dvh], gout[dvh], [("gout", dvh)], ("GT", 2 * h + dvh), f"gout{dvh}")
                if h + 1 < H:
                    sg_loads(h + 1)

        def phase_P3():
            enter_phase("A", AKEYS)
            a_load([GT[kc] for kc in range(RC)] + [YB[kc] for kc in range(CC)],
                   [("GT", kc) for kc in range(RC)] + [("YB", kc) for kc in range(CC)])
            o = X_OFF
            gr = []
            for i in range(3):
                gr.append(view(o, [128, T], F32)); o += ROW4
            tmp = []
            for i in range(4):
                tmp.append(view(o, [128, 512], F32)); o += 2048
            mrow = []
            for i in range(2):
                mrow.append(view(o, [128, T], BF16)); o += ROW2
            keys = [("gr", i) for i in range(3)] + [("tmp", i) for i in range(4)] + GK(("mrow", 0)) + GK(("mrow", 1))
            enter_phase("X", keys)
            grc = [0]
            tc = [0]
            pre = {}

            def prefetch(m):
                if m >= KC or m in pre:
                    return
                ia = grc[0] % 3
                ib = (grc[0] + 1) % 3
                grc[0] += 2
                load(gr[ia], GA[m], ("GA", m), [("gr", ia)], f"gr{ia}")
                load(gr[ib], GB[m], ("GB", m), [("gr", ib)], f"gr{ib}")
                pre[m] = (ia, ib)
            prefetch(0)
            for m in range(KC):
                ia, ib = pre[m]
                mi = m % 2
                mr = mrow[mi]

                def epi(g, off, n, bks, ia=ia, ib=ib, mr=mr, mi=mi):
                    t1 = tc[0] % 4
                    t2 = (tc[0] + 1) % 4
                    tc[0] += 2
                    P.op("vector", lambda e: e.tensor_tensor(out=tmp[t1][:, :n], in0=ps[bks[0]][:, :n], in1=gr[ia][:, off:off + n], op=ALU.mult),
                         reads=[("ps", bks[0]), ("gr", ia)], writes=[("tmp", t1)])
                    P.op("vector", lambda e: e.tensor_tensor(out=tmp[t2][:, :n], in0=ps[bks[1]][:, :n], in1=gr[ib][:, off:off + n], op=ALU.mult),
                         reads=[("ps", bks[1]), ("gr", ib)], writes=[("tmp", t2)])
                    P.op("vector", lambda e: e.tensor_tensor(out=mr[:, off:off + n], in0=tmp[t1][:, :n], in1=tmp[t2][:, :n], op=ALU.add),
                         reads=[("tmp", t1), ("tmp", t2)], writes=[(("mrow", mi), g)])
                lin_block(epi, kc_split=[list(range(RC)), list(range(RC, KC))], edge=(m == 0 or m == KC - 1))
                if m + 1 < KC:
                    prefetch(m + 1)
                store(MG[m], mr, GK(("mrow", mi)), ("MG", m), f"mrow{mi}")

        def phase_resid(src_dram, src_name, blk0, stats=False):
            enter_phase("A", AKEYS)
            a_load([src_dram[blk0 + kc] for kc in range(KC)], [(src_name, blk0 + kc) for kc in range(KC)])
            o = X_OFF
            xr = []
            for i in range(2):
                xr.append(view(o, [128, T], F32)); o += ROW4
            hr = []
            for i in range(2):
                hr.append(view(o, [128, T], F32)); o += ROW4
            acc = view(RSTD_OFF, [128, T], F32)
            keys = [("xr", 0), ("xr", 1)] + GK(("hr", 0)) + GK(("hr", 1))
            enter_phase("X", keys + (["rstd"] if stats else []))
            load(xr[0], R[0], ("R", 0), [("xr", 0)], "xr0")
            for m in range(KC):
                mi = m % 2
                if m + 1 < KC:
                    load(xr[(m + 1) % 2], R[m + 1], ("R", m + 1), [("xr", (m + 1) % 2)], f"xr{(m + 1) % 2}")

                def epi(g, off, n, bks, mi=mi):
                    P.op("vector", lambda e: e.tensor_tensor(out=hr[mi][:, off:off + n], in0=ps[bks[0]][:, :n],
                                                             in1=xr[mi][:, off:off + n], op=ALU.add),
                         reads=[("ps", bks[0]), ("xr", mi)], writes=[(("hr", mi), g)])
                lin_block(epi, edge=(m == 0 or m == KC - 1))
                store(R[m], hr[mi], GK(("hr", mi)), ("R", m), f"hr{mi}")
                if stats:
                    P.op("scalar", lambda e, mi=mi: e.activation(out=xr[mi], in_=hr[mi], func=AF.Square),
                         reads=GK(("hr", mi)), writes=[("xr", mi)])
                    if m == 0:
                        P.op("vector", lambda e, mi=mi: e.tensor_copy(out=acc, in_=xr[mi]), reads=[("xr", mi)], writes=["rstd"])
                    else:
                        P.op("vector", lambda e, mi=mi: e.tensor_tensor(out=acc, in0=acc, in1=xr[mi], op=ALU.add),
                             reads=[("xr", mi), "rstd"], writes=["rstd"])
            if stats:
                stats_finish(acc)

        def phase_P5():
            o = X_OFF
            tmp = []
            for i in range(4):
                tmp.append(view(o, [128, 512], F32)); o += 2048
            hrow = []
            for i in range(2):
                hrow.append(view(o, [128, T], BF16)); o += ROW2
            keys = [("tmp", i) for i in range(4)] + GK(("hrow", 0)) + GK(("hrow", 1))
            enter_phase("X", keys)
            tc = [0]
            for f in range(FC):
                hi = f % 2

                def epi(g, off, n, bks, hi=hi):
                    t1 = tc[0] % 4
                    tc[0] += 1
                    P.op("scalar", lambda e: e.activation(out=tmp[t1][:, :n], in_=ps[bks[0]][:, :n], func=AF.Relu),
                         reads=[("ps", bks[0])], writes=[("tmp", t1)])
                    P.op("vector", lambda e: e.tensor_tensor(out=hrow[hi][:, off:off + n], in0=tmp[t1][:, :n], in1=tmp[t1][:, :n], op=ALU.mult),
                         reads=[("tmp", t1)], writes=[(("hrow", hi), g)])
                lin_block(epi, edge=(f == FC - 1))
                store(HID[f], hrow[hi], GK(("hrow", hi)), ("HID", f), f"hrow{hi}")

        def phase_final():
            _, _, rstd = phase_norm(2 * L, have_stats=True, final=True)
            KQ = 4
            need = 2 * KQ * ROW4 + 2 * al(NCH * KQ * 128 * 4)
            if A_OFF + need <= RSTD_OFF:
                fo = A_OFF
            else:
                fo = RSTD_OFF + ROW4
                assert fo + need <= ARENA, (fo, need)
            xq = [[view(fo + (b * KQ + j) * ROW4, [128, T], F32) for j in range(KQ)] for b in range(2)]
            fo2 = fo + 2 * KQ * ROW4
            ost = [view(fo2 + b * al(NCH * KQ * 128 * 4), [128, NCH, KQ * 128], F32) for b in range(2)]
            keys = [("xq", 0), ("xq", 1), ("ost", 0), ("ost", 1)]
            P.alias(keys, [k for k in region_keys["A"] + region_keys["X"] if k != "rstd"])
            gi = 2 * L
            toks = []
            for kq in range(KC // KQ):
                b = kq % 2
                P.op("sync", lambda e, kq=kq, b=b: [e.dma_start(out=xq[b][j], in_=R[kq * KQ + j]) for j in range(KQ)],
                     reads=[("R", kq * KQ + j) for j in range(KQ)], writes=[("xq", b)], dma=f"xq{b}", ndma=KQ)
                for j in range(KQ):
                    kc = kq * KQ + j
                    P.op("vector", lambda e, b=b, j=j, kc=kc: e.scalar_tensor_tensor(
                        out=xq[b][j], in0=xq[b][j], scalar=gains[:, gi, kc:kc + 1], in1=rstd, op0=ALU.mult, op1=ALU.mult),
                        reads=[("xq", b), "rstd", "gains"], writes=[("xq", b)])
                for c in range(NCH):
                    bk = banks(1)[0]
                    t0 = T0 + c * 128

                    def tr(e, b=b, t0=t0, bk=bk):
                        last = None
                        for j in range(KQ):
                            last = e.transpose(out=ps[bk][:, j * 128:(j + 1) * 128], in_=xq[b][j][:, t0:t0 + 128], identity=ident)
                        return last
                    P.op("tensor", tr, reads=[("xq", b), "ident"], writes=[("ps", bk)])
                    dst = ost[b][:, c, :]
                    srcp = ps[bk][:, :KQ * 128]
                    if c % 2 == 0:
                        P.op("vector", lambda e, dst=dst, srcp=srcp: e.tensor_copy(out=dst, in_=srcp),
                             reads=[("ps", bk)], writes=[("ost", b)])
                    else:
                        P.op("scalar", lambda e, dst=dst, srcp=srcp: e.activation(out=dst, in_=srcp, func=AF.Copy),
                             reads=[("ps", bk)], writes=[("ost", b)])
                dsty = y_out[:, kq * KQ * 128:(kq + 1) * KQ * 128].rearrange("(c p) f -> p c f", p=128)
                t = P.op("gpsimd", lambda e, dsty=dsty, b=b: [e.dma_start(out=dsty, in_=ost[b])],
                         reads=[("ost", b)], writes=[("y", kq)], dma=f"ost{b}")
                toks.append(t)
            P.wait_all("gpsimd", toks)

        stages = [("T0", phase_T0)]
        for l in range(L):
            stages += [("N1", lambda l=l: phase_norm(2 * l, have_stats=(l > 0))), ("P1", lambda l=l: phase_P1(l)), ("P2", phase_P2),
                       ("P3", phase_P3), ("P4", lambda: phase_resid(MG, "MG", 0, stats=True)),
                       ("N2", lambda l=l: phase_norm(2 * l + 1, have_stats=True)), ("P5", phase_P5)]
            for qd in range(NQ):
                stages.append(("P6", lambda qd=qd: phase_resid(HID, "HID", qd * KC, stats=(qd == NQ - 1))))
        stages.append(("F", phase_final))
        if limit is not None:
            stages = stages[:limit]
        for nm, fn in stages:
            fn()
        if limit is None:
            assert wstate["next_use"] == NTOT, (wstate, NTOT)
        P.emit()
    return nc


_CACHE = {}


def run(cfg, x, meta_tokens, norm_mix_g, w_in, conv_w, w_ret_out, w_conv_out, w_out,
        norm_mlp_g, w_up, w_down, final_norm_g, n_cores=None):
    x = np.asarray(x)
    B = x.shape[0]
    n_cores = B if n_cores is None else n_cores
    key = (cfg.D, cfg.H, cfg.DFF, cfg.NCH, cfg.depth)
    if key not in _CACHE:
        _CACHE[key] = build(cfg)
    nc = _CACHE[key]
    ct = const_tables(cfg)
    gains, cw = prep_small(cfg, norm_mix_g, norm_mlp_g, final_norm_g, conv_w)
    shared = {
        "meta": np.ascontiguousarray(np.asarray(meta_tokens, dtype=np.float32)),
        "gains": gains.reshape(128, -1), "cw": cw.reshape(128, -1),
        "cs": ct["cs"], "maskT": ct["maskT"].reshape(128, -1), "QD": ct["QD"], "KD": ct["KD"].reshape(128, -1),
        "ident": ct["ident"],
    }
    for l in range(cfg.depth):
        shared[f"w{l}"] = prep_weights(cfg, l, w_in, w_ret_out, w_conv_out, w_out, w_up, w_down)
    in_maps = []
    for b in range(n_cores):
        m = dict(shared)
        m["x"] = np.ascontiguousarray(x[b])
        in_maps.append(m)
    res = run_bass_kernel_spmd(nc, in_maps, core_ids=list(range(n_cores)))
    return np.stack([np.asarray(r["y"]) for r in res.results], axis=0).astype(np.float32)


def kernel(x, meta_tokens, norm_mix_g, w_in, conv_w, w_ret_out, w_conv_out, w_out,
           norm_mlp_g, w_up, w_down, final_norm_g):
    cfg = Cfg(**FULL)
    return run(cfg, x, meta_tokens, norm_mix_g, w_in, conv_w, w_ret_out, w_conv_out, w_out,
               norm_mlp_g, w_up, w_down, final_norm_g)
```

```python
import contextlib
import numpy as np
import concourse.bass as bass
import concourse.mybir as mybir
from concourse.bass_utils import run_bass_kernel_spmd

F32 = mybir.dt.float32
BF16 = mybir.dt.bfloat16
U8 = mybir.dt.uint8
ALU = mybir.AluOpType
AF = mybir.ActivationFunctionType
ENGS = ["sync", "scalar", "vector", "gpsimd", "tensor"]
EPS = 1e-6
ROPE_BASE = 10000.0
NW = 3
P2SUB = 9
PASS2_SPLIT = 0


class Prog:
    def __init__(self, nc, es):
        self.nc = nc
        self.es = es
        self.ops = {e: [] for e in ENGS}
        self.sems = {}
        self.esem = {e: self._newsem("p_" + e) for e in ENGS}
        self.ecount = {e: 0 for e in ENGS}
        self.waited = {e: {} for e in ENGS}
        self.dsem = {}
        self.lastw = {}
        self.readers = {}
        self.nops = 0

    def _newsem(self, name):
        s = self.es.enter_context(self.nc.semaphore(name))
        self.sems[id(s)] = s
        return s

    def op(self, eng, fn, reads=(), writes=(), dma=None, ndma=1):
        deps = {}
        for k in reads:
            t = self.lastw.get(k)
            if t is not None:
                deps[t[0]] = max(deps.get(t[0], 0), t[1])
        for k in writes:
            t = self.lastw.get(k)
            if t is not None:
                deps[t[0]] = max(deps.get(t[0], 0), t[1])
            for sid, v in self.readers.get(k, {}).items():
                deps[sid] = max(deps.get(sid, 0), v)
        waits = []
        wd = self.waited[eng]
        own = id(self.esem[eng])
        for sid, v in deps.items():
            if sid == own and eng == "tensor":
                continue
            if wd.get(sid, 0) >= v:
                continue
            wd[sid] = v
            waits.append((self.sems[sid], v))
        if dma is not None:
            if dma not in self.dsem:
                self.dsem[dma] = [self._newsem("d_" + dma), 0]
            ent = self.dsem[dma]
            ent[1] += 16 * ndma
            tok = (id(ent[0]), ent[1])
            inc = (ent[0], 16)
        else:
            self.ecount[eng] += 1
            tok = (own, self.ecount[eng])
            inc = (self.esem[eng], 1)
        self.ops[eng].append((waits, fn, inc, dma is not None))
        for k in writes:
            self.lastw[k] = tok
            self.readers[k] = {}
        for k in reads:
            r = self.readers.setdefault(k, {})
            r[tok[0]] = max(r.get(tok[0], 0), tok[1])
        self.nops += 1
        return tok

    def alias(self, new_keys, old_keys):
        merged = {}
        for k in old_keys:
            t = self.lastw.get(k)
            if t is not None:
                merged[t[0]] = max(merged.get(t[0], 0), t[1])
            for sid, v in self.readers.get(k, {}).items():
                merged[sid] = max(merged.get(sid, 0), v)
        for k in new_keys:
            self.lastw[k] = None
            self.readers[k] = dict(merged)

    def wait_all(self, eng, toks):
        waits = []
        wd = self.waited[eng]
        for sid, v in toks:
            if wd.get(sid, 0) >= v:
                continue
            wd[sid] = v
            waits.append((self.sems[sid], v))
        self.ops[eng].append((waits, None, None, False))

    def emit(self):
        with self.nc.Block() as block:
            def mk(engname):
                def body(e):
                    for waits, fn, inc, isdma in self.ops[engname]:
                        for sem, v in waits:
                            e.wait_ge(sem, v)
                        if fn is None:
                            continue
                        r = fn(e)
                        if isdma:
                            for ins in r:
                                ins.then_inc(inc[0], inc[1])
                        else:
                            r.then_inc(inc[0], inc[1])
                return body
            block.sync(mk("sync"))
            block.scalar(mk("scalar"))
            block.vector(mk("vector"))
            block.gpsimd(mk("gpsimd"))
            block.tensor(mk("tensor"))


class Cfg:
    def __init__(self, D, H, DFF, NCH, depth, T0=16):
        self.D, self.H, self.DFF, self.NCH, self.depth, self.T0 = D, H, DFF, NCH, depth, T0
        self.DH = 256
        self.DR = H * 256
        self.DC = self.DR
        assert D == 2 * self.DR
        self.KC = D // 128
        self.RC = self.DR // 128
        self.CC = self.DC // 128
        self.FC = DFF // 128
        self.NQ = DFF // D
        self.T = T0 + 128 * NCH
        assert NCH % 4 == 0
        ng0 = -(-self.T // 512)
        base = -(-(-(-self.T // ng0)) // 16) * 16
        self.groups = [(i * base, min(base, self.T - i * base)) for i in range(ng0)]
        ng = len(self.groups)
        self.sets = [list(range(i, min(i + 3, ng))) for i in range(0, ng, 3)]
        self.chunks = [(0, T0)] + [(T0 + 128 * i, 128) for i in range(NCH)]
        self.NIN = 4 * self.RC + 3 * self.CC + 2 * self.KC
        self.NBLK = self.NIN + self.KC + self.KC + self.FC + self.NQ * self.KC
        order = []
        for h in range(H):
            for nm in ("q", "k", "v", "g"):
                for j in range(2):
                    order.append((nm, h, j))
        for i in range(self.CC):
            order += [("c", i, 0), ("z", i, 0), ("b", i, 0)]
        for m in range(self.KC):
            order.append(("ga", m, 0))
        for m in range(self.KC):
            order.append(("gb", m, 0))
        self.in_order = order

    def in_src_block(self, ent):
        nm, a, j = ent
        RC, CC, KC = self.RC, self.CC, self.KC
        if nm == "q":
            return 2 * a + j
        if nm == "k":
            return RC + 2 * a + j
        if nm == "v":
            return 2 * RC + 2 * a + j
        if nm == "g":
            return 3 * RC + 2 * a + j
        if nm == "c":
            return 4 * RC + a
        if nm == "b":
            return 4 * RC + CC + a
        if nm == "z":
            return 4 * RC + 2 * CC + a
        if nm == "ga":
            return 4 * RC + 3 * CC + a
        if nm == "gb":
            return 4 * RC + 3 * CC + KC + a
        raise ValueError(nm)


FULL = dict(D=4096, H=8, DFF=16384, NCH=16, depth=2)


def blockify(W):
    K, M = W.shape
    return np.ascontiguousarray(
        W.reshape(K // 128, 128, M // 128, 128).transpose(2, 1, 0, 3)).reshape(M // 128, 128, (K // 128) * 128)


def prep_weights(cfg, l, w_in, w_ret_out, w_conv_out, w_out, w_up, w_down):
    out = np.empty((cfg.NBLK, 128, cfg.KC * 128), np.float32)
    wb = blockify(np.asarray(w_in[l]))
    idx = [cfg.in_src_block(e) for e in cfg.in_order]
    n = 0
    out[n:n + cfg.NIN] = wb[idx]
    n += cfg.NIN
    del wb
    out[n:n + cfg.KC] = blockify(np.concatenate([np.asarray(w_ret_out[l]), np.asarray(w_conv_out[l])], axis=0))
    n += cfg.KC
    out[n:n + cfg.KC] = blockify(np.asarray(w_out[l]))
    n += cfg.KC
    out[n:n + cfg.FC] = blockify(np.asarray(w_up[l]))
    n += cfg.FC
    wd = np.asarray(w_down[l])
    for qd in range(cfg.NQ):
        out[n:n + cfg.KC] = blockify(wd[qd * cfg.D:(qd + 1) * cfg.D])
        n += cfg.KC
    assert n == cfg.NBLK
    return out


def const_tables(cfg):
    T, H = cfg.T, cfg.H
    d = 256
    inv_freq = (ROPE_BASE ** (-np.arange(0, d, 2, dtype=np.float32) / d)).astype(np.float32)
    pos = np.arange(T, dtype=np.float32)
    ang = (pos[None, :] * inv_freq[:, None]).astype(np.float32)
    cs = np.stack([np.cos(ang), np.sin(ang)]).astype(np.float32)
    gam = 1.0 - np.exp2(-5.0 - np.arange(H, dtype=np.float64))
    lg = np.log(gam)
    scale = 256.0 ** -0.5
    idx = np.arange(128, dtype=np.float64)
    rel = idx[None, :] - idx[:, None]
    maskT = np.where(rel[None] >= 0, np.exp(lg[:, None, None] * np.maximum(rel, 0.0)[None]), 0.0) * scale
    j = (np.arange(T) + (128 - cfg.T0)) % 128
    qd = np.exp(lg[:, None] * (j[None, :] + 1.0))
    QD = np.broadcast_to(qd[:, None, :], (H, 128, T))
    KD = np.zeros((H, 128, 2))
    p = np.arange(128)
    KD[:, :, 1] = np.exp(lg[:, None] * (127.0 - p[None, :])) * scale
    KD[:, :cfg.T0, 0] = np.exp(lg[:, None] * (cfg.T0 - 1.0 - p[None, :cfg.T0])) * scale
    cd = np.exp(lg * 128.0)
    return dict(cs=cs, maskT=np.ascontiguousarray(maskT.astype(np.float32).transpose(1, 0, 2)),
                QD=np.ascontiguousarray(QD.astype(np.float32)),
                KD=np.ascontiguousarray(KD.astype(np.float32).transpose(1, 0, 2)),
                cd=[float(np.float32(c)) for c in cd],
                ident=np.eye(128, dtype=np.float32))


def prep_small(cfg, norm_mix_g, norm_mlp_g, final_norm_g, conv_w):
    KC, CC, L = cfg.KC, cfg.CC, cfg.depth
    gains = np.zeros((128, 2 * L + 1, KC), np.float32)
    for l in range(L):
        gains[:, 2 * l, :] = np.asarray(norm_mix_g[l]).reshape(KC, 128).T
        gains[:, 2 * l + 1, :] = np.asarray(norm_mlp_g[l]).reshape(KC, 128).T
    gains[:, 2 * L, :] = np.asarray(final_norm_g).reshape(KC, 128).T
    cw = np.zeros((128, L, CC, 3), np.float32)
    for l in range(L):
        cw[:, l] = np.asarray(conv_w[l]).reshape(3, CC, 128).transpose(2, 1, 0)
    return gains, cw


def build(cfg, limit=None):
    D, H, KC, RC, CC, FC, NQ, T, T0, NCH, L = (cfg.D, cfg.H, cfg.KC, cfg.RC, cfg.CC, cfg.FC, cfg.NQ,
                                               cfg.T, cfg.T0, cfg.NCH, cfg.depth)
    SEQ = 128 * NCH
    GROUPS, SETS, CHUNKS = cfg.groups, cfg.sets, cfg.chunks
    NG = len(GROUPS)
    consts = const_tables(cfg)
    CD = consts["cd"]

    nc = bass.Bass("TRN2", target_bir_lowering=False)
    dt_in = lambda name, shape: nc.dram_tensor(name, shape, F32, kind="ExternalInput").ap()
    x_in = dt_in("x", [SEQ, D])
    meta_in = dt_in("meta", [T0, D])
    W_in = [dt_in(f"w{l}", [cfg.NBLK, 128, KC * 128]) for l in range(L)]
    gains_in = dt_in("gains", [128, (2 * L + 1) * KC])
    cw_in = dt_in("cw", [128, L * CC * 3])
    cs_in = dt_in("cs", [2, 128, T])
    mask_in = dt_in("maskT", [128, H * 128])
    qd_in = dt_in("QD", [H, 128, T])
    kd_in = dt_in("KD", [128, H * 2])
    ident_in = dt_in("ident", [128, 128])
    y_out = nc.dram_tensor("y", [SEQ, D], F32, kind="ExternalOutput").ap()

    scr = lambda name, shape, dt: nc.dram_tensor(name, shape, dt, kind="Internal").ap()
    R = scr("R", [KC, 128, T], F32)
    QKV = scr("QKV", [H, 6, 128, T], F32)
    SG = scr("SG", [RC, 128, T], F32)
    YB = scr("YB", [CC, 128, T], BF16)
    GA = scr("GA", [KC, 128, T], F32)
    GB = scr("GB", [KC, 128, T], F32)
    GT = scr("GT", [RC, 128, T], BF16)
    MG = scr("MG", [KC, 128, T], BF16)
    HID = scr("HID", [FC, 128, T], BF16)

    es = contextlib.ExitStack()
    with es:
        P = Prog(nc, es)
        ARENA = 211000
        arena = es.enter_context(nc.sbuf_tensor("arena", [128, ARENA], U8)).ap()

        def view(off, shape, dt):
            nb = int(np.prod(shape[1:])) * (4 if dt == F32 else 2)
            assert off % 32 == 0, off
            assert off + nb <= ARENA, (off, nb)
            v = arena[:, off:off + nb].bitcast(dt)
            if len(shape) == 3:
                v = v.rearrange("p (a b) -> p a b", a=shape[1])
            return v

        al = lambda n: (n + 63) // 64 * 64
        ROW4 = al(T * 4)
        ROW2 = al(T * 2)
        WSZ = KC * 128 * 2
        W_OFF = 0
        wbf = [view(W_OFF + i * WSZ, [128, KC, 128], BF16) for i in range(NW)]
        C_OFF = W_OFF + NW * WSZ
        co = [C_OFF]

        def calloc(shape, dt):
            nb = al(int(np.prod(shape[1:])) * (4 if dt == F32 else 2))
            v = view(co[0], shape, dt)
            co[0] += nb
            return v
        ident = calloc([128, 128], F32)
        onesD = calloc([128, 128], F32)
        ones256 = calloc([128, 128], F32)
        gains = calloc([128, (2 * L + 1), KC], F32)
        cwt = calloc([128, L * CC * 3], F32)
        kdt = calloc([128, H * 2], F32)
        maskT = calloc([128, H, 128], F32)
        A_OFF = al(co[0])
        A_SZ = al(KC * T * 2)
        Abuf = view(A_OFF, [128, KC, T], BF16)
        X_OFF = A_OFF + A_SZ
        X_SZ = ARENA - X_OFF
        assert X_SZ >= 5 * ROW4 + 64, X_SZ

        ps = [es.enter_context(nc.psum_tensor(f"ps{i}", [128, 512], F32)) for i in range(8)]
        bankc = [0]

        def banks(n):
            r = [(bankc[0] + i) % 8 for i in range(n)]
            bankc[0] = (bankc[0] + n) % 8
            return r

        region_keys = {"A": [], "X": []}

        def enter_phase(region, keys):
            P.alias(keys, region_keys[region])
            region_keys[region] = list(keys)

        def enter_both(keys):
            P.alias(keys, region_keys["A"] + region_keys["X"])
            region_keys["A"] = list(keys)
            region_keys["X"] = list(keys)

        AKEYS = [("A", kc) for kc in range(KC)]
        enter_phase("A", AKEYS)

        def ld(dst, src, key):
            P.op("sync", lambda e: [e.dma_start(out=dst, in_=src)], writes=[key], dma="c_" + str(key))
        ld(ident, ident_in, "ident")
        ld(gains.rearrange("p a b -> p (a b)"), gains_in, "gains")
        ld(cwt, cw_in, "cw")
        ld(kdt, kd_in, "kdt")
        ld(maskT.rearrange("p a b -> p (a b)"), mask_in, "maskT")
        P.op("vector", lambda e: e.memset(onesD, 1.0 / D), writes=["onesD"])
        P.op("vector", lambda e: e.memset(ones256, 1.0 / 256), writes=["ones256"])

        NTOT = L * cfg.NBLK
        wstate = {"next_load": 0, "next_use": 0}

        def w_load():
            j = wstate["next_load"]
            if j >= NTOT:
                return
            wstate["next_load"] += 1
            l, b = divmod(j, cfg.NBLK)
            dst = wbf[j % NW].rearrange("p a b -> p (a b)")
            src = W_in[l][b]
            P.op("gpsimd", lambda e: [e.dma_start(out=dst, in_=src)], writes=[("w", j % NW)], dma=f"w{j % NW}")

        def w_next():
            j = wstate["next_use"]
            wstate["next_use"] += 1
            while wstate["next_load"] < min(NTOT, j + NW):
                w_load()
            return wbf[j % NW], ("w", j % NW)

        APARTS = 4
        assert KC % APARTS == 0
        APSZ = KC // APARTS

        def a_load(srcs, rkeys):
            for p in range(APARTS):
                kcs = list(range(p * APSZ, (p + 1) * APSZ))
                P.op("gpsimd", lambda e, kcs=kcs: [e.dma_start(out=Abuf[:, kc, :], in_=srcs[kc]) for kc in kcs],
                     reads=[rkeys[kc] for kc in kcs], writes=[("A", kc) for kc in kcs], dma=f"Aload{p}", ndma=len(kcs))

        def lin_set(wb, wkey, st, accs, parts):
            for kcs, bk in accs:
                n_kc = len(kcs)
                psz = -(-n_kc // parts)
                for p0 in range(0, n_kc, psz):
                    sub = kcs[p0:p0 + psz]

                    def mm(e, sub=sub, p0=p0, bk=bk, n_kc=n_kc):
                        last = None
                        for i, kc in enumerate(sub):
                            for g in st:
                                off, n = GROUPS[g]
                                last = e.matmul(ps[bk[g]][:, :n], lhsT=wb[:, kc, :], rhs=Abuf[:, kc, off:off + n],
                                                start=(p0 + i == 0), stop=(p0 + i == n_kc - 1))
                        return last
                    P.op("tensor", mm, reads=[wkey] + [("A", kc) for kc in sub], writes=[("ps", bk[g]) for g in st])

        def lin_block(epi, kc_split=None, edge=False):
            wb, wkey = w_next()
            kls = [list(range(KC))] if kc_split is None else kc_split
            parts = (APARTS // len(kls)) if edge else 1
            sets = [list(range(NG))] if (edge and len(kls) == 1 and NG <= 6) else SETS
            for st in sets:
                accs = []
                for kcs in kls:
                    b = banks(len(st))
                    accs.append((kcs, {g: b[i] for i, g in enumerate(st)}))
                lin_set(wb, wkey, st, accs, parts)
                for g in st:
                    off, n = GROUPS[g]
                    epi(g, off, n, [bk[g] for _, bk in accs])

        evc = [0]

        def evac_copy(dst, bank, n, rkeys, wkeys, eng=None):
            if eng is None:
                eng = "scalar" if evc[0] % 2 == 0 else "vector"
                evc[0] += 1
            src = ps[bank][:, :n]
            if eng == "scalar":
                P.op("scalar", lambda e: e.activation(out=dst, in_=src, func=AF.Copy), reads=rkeys, writes=wkeys)
            else:
                P.op("vector", lambda e: e.tensor_copy(out=dst, in_=src), reads=rkeys, writes=wkeys)

        def act_evac(dst, bank, n, func, rkeys, wkeys):
            src = ps[bank][:, :n]
            P.op("scalar", lambda e: e.activation(out=dst, in_=src, func=func), reads=rkeys, writes=wkeys)

        def store(dst_dram, src, rkeys, dkey, chan):
            return P.op("sync", lambda e: [e.dma_start(out=dst_dram, in_=src)], reads=rkeys, writes=[dkey], dma=chan)

        def load(dst, src_dram, dkey, wkeys, chan, eng="sync"):
            return P.op(eng, lambda e: [e.dma_start(out=dst, in_=src_dram)], reads=[dkey], writes=wkeys, dma=chan)

        GK = lambda name: [(name, g) for g in range(NG)]

        def phase_T0():
            QT = 4
            XT = [view(A_OFF + i * al(D * 4), [128, D], F32) for i in range(2)]
            base = A_OFF + 2 * al(D * 4)
            XS = [view(base + i * al(KC * QT * 128 * 4), [128, KC, QT * 128], F32) for i in range(2)]
            assert base + 2 * al(KC * QT * 128 * 4) <= ARENA
            keys = [("xt", 0), ("xt", 1), ("xs", 0), ("xs", 1)]
            enter_both(keys)
            tiles = [[0]] + [list(range(1 + q, 1 + min(q + QT, NCH))) for q in range(0, NCH, QT)]
            li = 0
            for ti, tl in enumerate(tiles):
                xs = XS[ti % 2]
                tbase = CHUNKS[tl[0]][0]
                for ci in tl:
                    t0, c = CHUNKS[ci]
                    xt = XT[li % 2]
                    src = meta_in[:, :] if ci == 0 else x_in[(ci - 1) * 128: ci * 128, :]
                    P.op("sync", lambda e, xt=xt, src=src, c=c: [e.dma_start(out=xt[:c, :], in_=src)],
                         writes=[("xt", li % 2)], dma=f"xt{li % 2}")
                    lo = t0 - tbase
                    for k0 in range(0, KC, 4):
                        nk = min(4, KC - k0)
                        bk = banks(1)[0]

                        def tr(e, xt=xt, k0=k0, nk=nk, bk=bk, c=c):
                            last = None
                            for i in range(nk):
                                last = e.transpose(out=ps[bk][:, i * c:(i + 1) * c],
                                                   in_=xt[:c, (k0 + i) * 128:(k0 + i + 1) * 128], identity=ident[:c, :c])
                            return last
                        P.op("tensor", tr, reads=[("xt", li % 2), "ident"], writes=[("ps", bk)])
                        dst = xs[:, k0:k0 + nk, lo:lo + c]
                        src_ps = ps[bk][:, :nk * c].rearrange("p (a b) -> p a b", a=nk)
                        if (k0 // 4) % 2 == 0:
                            P.op("vector", lambda e, dst=dst, src_ps=src_ps: e.tensor_copy(out=dst, in_=src_ps),
                                 reads=[("ps", bk)], writes=[("xs", ti % 2)])
                        else:
                            P.op("scalar", lambda e, dst=dst, src_ps=src_ps: e.activation(out=dst, in_=src_ps, func=AF.Copy),
                                 reads=[("ps", bk)], writes=[("xs", ti % 2)])
                    li += 1
                ntok = sum(CHUNKS[ci][1] for ci in tl)
                dstR = R[:, :, tbase:tbase + ntok].rearrange("k p t -> p k t")
                P.op("gpsimd", lambda e, dstR=dstR, xs=xs, ntok=ntok: [e.dma_start(out=dstR, in_=xs[:, :, :ntok])],
                     reads=[("xs", ti % 2)], writes=[("R", kc) for kc in range(KC)], dma=f"xs{ti % 2}")

        RSTD_OFF = X_OFF + 4 * ROW4

        def enter_X_keep_rstd(keys):
            P.alias(keys, [k for k in region_keys["X"] if k != "rstd"])
            region_keys["X"] = list(keys) + ["rstd"]

        def stats_finish(acc):
            sb = banks(NG)

            def mm(e):
                last = None
                for g, (off, n) in enumerate(GROUPS):
                    last = e.matmul(ps[sb[g]][:, :n], lhsT=onesD, rhs=acc[:, off:off + n], start=True, stop=True)
                return last
            P.op("tensor", mm, reads=["rstd", "onesD"], writes=[("ps", b) for b in sb])
            for g, (off, n) in enumerate(GROUPS):
                P.op("scalar", lambda e, g=g, off=off, n=n: e.activation(out=acc[:, off:off + n], in_=ps[sb[g]][:, :n],
                                                                      func=AF.Sqrt, bias=EPS, scale=1.0),
                     reads=[("ps", sb[g])], writes=["rstd"])
            P.op("vector", lambda e: e.reciprocal(out=acc, in_=acc), reads=["rstd"], writes=["rstd"])

        def phase_norm(gi, have_stats=False, final=False):
            NXC = 4 if (have_stats and not final) else 2
            xc = [view(X_OFF + i * ROW4, [128, T], F32) for i in range(NXC)]
            sq = [view(X_OFF + (2 + i) * ROW4, [128, T], F32) for i in range(2)]
            rstd = view(RSTD_OFF, [128, T], F32)
            keys = [("xc", i) for i in range(NXC)] + ([("sq", 0), ("sq", 1)] if NXC == 2 else [])
            if have_stats:
                enter_X_keep_rstd(keys)
            else:
                enter_phase("X", keys + ["rstd"])
                sb = banks(NG)
                for kc in range(KC):
                    x_ = xc[kc % NXC]
                    s_ = sq[kc % 2]
                    load(x_, R[kc], ("R", kc), [("xc", kc % NXC)], f"xc{kc % NXC}")
                    P.op("scalar", lambda e, x_=x_, s_=s_: e.activation(out=s_, in_=x_, func=AF.Square),
                         reads=[("xc", kc % NXC)], writes=[("sq", kc % 2)])

                    def mm(e, s_=s_, kc=kc):
                        last = None
                        for g, (off, n) in enumerate(GROUPS):
                            last = e.matmul(ps[sb[g]][:, :n], lhsT=onesD, rhs=s_[:, off:off + n],
                                            start=(kc == 0), stop=(kc == KC - 1))
                        return last
                    P.op("tensor", mm, reads=[("sq", kc % 2), "onesD"], writes=[("ps", b) for b in sb])
                for g, (off, n) in enumerate(GROUPS):
                    P.op("scalar", lambda e, g=g, off=off, n=n: e.activation(out=rstd[:, off:off + n], in_=ps[sb[g]][:, :n],
                                                                          func=AF.Sqrt, bias=EPS, scale=1.0),
                         reads=[("ps", sb[g])], writes=["rstd"])
                P.op("vector", lambda e: e.reciprocal(out=rstd, in_=rstd), reads=["rstd"], writes=["rstd"])
            if final:
                return xc, sq, rstd
            if NXC == 2:
                rows2 = [(xc[0], ("xc", 0), "xc0"), (xc[1], ("xc", 1), "xc1"), (sq[0], ("sq", 0), "sqld0"), (sq[1], ("sq", 1), "sqld1")]
            else:
                rows2 = [(xc[i], ("xc", i), f"xc{i}") for i in range(NXC)]
            for kc in range(KC):
                x_, xk, xch = rows2[kc % len(rows2)]
                load(x_, R[kc], ("R", kc), [xk], xch)
                if PASS2_SPLIT and NXC == 4 and kc % 4 == 3:
                    P.op("scalar", lambda e, x_=x_, kc=kc: e.activation(out=x_, in_=x_, func=AF.Identity, scale=gains[:, gi, kc:kc + 1]),
                         reads=[("xc", kc % NXC), "gains"], writes=[("xc", kc % NXC)])
                    P.op("gpsimd", lambda e, x_=x_, kc=kc: e.tensor_tensor(out=Abuf[:, kc, :], in0=x_, in1=rstd, op=ALU.mult),
                         reads=[("xc", kc % NXC), "rstd"], writes=[("A", kc)])
                else:
                    P.op("vector", lambda e, x_=x_, kc=kc: e.scalar_tensor_tensor(
                        out=Abuf[:, kc, :], in0=x_, scalar=gains[:, gi, kc:kc + 1], in1=rstd, op0=ALU.mult, op1=ALU.mult),
                        reads=[xk, "rstd", "gains"], writes=[("A", kc)])

        def phase_P1(l):
            o = X_OFF
            crow = view(o, [128, T], F32); o += ROW4
            czrow = view(o, [128, T + 2], F32); o += al((T + 2) * 4)
            orow = []
            for i in range(2):
                orow.append(view(o, [128, T], F32)); o += ROW4
            brow = []
            for i in range(2):
                brow.append(view(o, [128, T], BF16)); o += ROW2
            keys = GK("crow") + GK("czrow") + ["czpad"] + GK(("orow", 0)) + GK(("orow", 1)) + GK(("brow", 0)) + GK(("brow", 1))
            enter_phase("X", keys)
            P.op("vector", lambda e: e.memset(czrow[:, 0:2], 0.0), writes=["czpad"])
            oc = [0]
            bc = [0]
            for ent in cfg.in_order:
                nm, a, j = ent
                if nm in ("q", "k", "v", "g", "ga", "gb"):
                    oi = oc[0] % 2
                    oc[0] += 1
                    orw = orow[oi]
                    if nm in ("q", "k", "v"):
                        dst = QKV[a, {"q": 0, "k": 2, "v": 4}[nm] + j]
                        dkey = ("QKV", a, {"q": 0, "k": 2, "v": 4}[nm] + j)
                        func = None
                    elif nm == "g":
                        dst = SG[2 * a + j]
                        dkey = ("SG", 2 * a + j)
                        func = AF.Silu
                    elif nm == "ga":
                        dst = GA[a]
                        dkey = ("GA", a)
                        func = AF.Sigmoid
                    else:
                        dst = GB[a]
                        dkey = ("GB", a)
                        func = AF.Sigmoid

                    def epi(g, off, n, bks, orw=orw, oi=oi, func=func):
                        if func is None:
                            evac_copy(orw[:, off:off + n], bks[0], n, [("ps", bks[0])], [(("orow", oi), g)])
                        else:
                            act_evac(orw[:, off:off + n], bks[0], n, func, [("ps", bks[0])], [(("orow", oi), g)])
                    lin_block(epi, edge=(ent == cfg.in_order[0]))
                    store(dst, orw, GK(("orow", oi)), dkey, f"orow{oi}")
                elif nm == "c":
                    def epi(g, off, n, bks):
                        evac_copy(crow[:, off:off + n], bks[0], n, [("ps", bks[0])], [("crow", g)])
                    lin_block(epi)
                elif nm == "z":
                    def epi(g, off, n, bks):
                        src = ps[bks[0]][:, :n]
                        P.op("vector", lambda e: e.tensor_tensor(out=czrow[:, 2 + off:2 + off + n], in0=src,
                                                                 in1=crow[:, off:off + n], op=ALU.mult),
                             reads=[("ps", bks[0]), ("crow", g)], writes=[("czrow", g)])
                    lin_block(epi)
                    cwb = (l * CC + a) * 3
                    P.op("vector", lambda e, cwb=cwb: e.tensor_scalar(out=crow, in0=czrow[:, 2:2 + T], scalar1=cwt[:, cwb + 2:cwb + 3],
                                                                      scalar2=None, op0=ALU.mult),
                         reads=GK("czrow") + ["cw"], writes=GK("crow"))
                    P.op("vector", lambda e, cwb=cwb: e.scalar_tensor_tensor(out=crow, in0=czrow[:, 1:1 + T], scalar=cwt[:, cwb + 1:cwb + 2],
                                                                             in1=crow, op0=ALU.mult, op1=ALU.add),
                         reads=GK("czrow") + ["czpad", "cw"] + GK("crow"), writes=GK("crow"))
                    P.op("vector", lambda e, cwb=cwb: e.scalar_tensor_tensor(out=crow, in0=czrow[:, 0:T], scalar=cwt[:, cwb:cwb + 1],
                                                                             in1=crow, op0=ALU.mult, op1=ALU.add),
                         reads=GK("czrow") + ["czpad", "cw"] + GK("crow"), writes=GK("crow"))
                else:
                    bi = bc[0] % 2
                    bc[0] += 1
                    brw = brow[bi]

                    def epi(g, off, n, bks, brw=brw, bi=bi):
                        src = ps[bks[0]][:, :n]
                        P.op("vector", lambda e: e.tensor_tensor(out=brw[:, off:off + n], in0=src,
                                                                 in1=crow[:, off:off + n], op=ALU.mult),
                             reads=[("ps", bks[0]), ("crow", g)], writes=[(("brow", bi), g)])
                    lin_block(epi)
                    store(YB[a], brw, GK(("brow", bi)), ("YB", a), f"brow{bi}")

        def phase_P2():
            o = [A_OFF]

            def ralloc(shape, dt):
                nb = al(int(np.prod(shape[1:])) * (4 if dt == F32 else 2))
                v = view(o[0], shape, dt)
                o[0] += nb
                return v
            cosT = ralloc([128, T], F32)
            sinT = ralloc([128, T], F32)
            qkv = [ralloc([128, T], F32) for _ in range(6)]
            taq = ralloc([128, T], F32)
            tak = ralloc([128, T], F32)
            tb = ralloc([128, T], F32)
            qr = [ralloc([128, T], BF16) for _ in range(2)]
            qd = [ralloc([128, T], BF16) for _ in range(2)]
            kr = [ralloc([128, T], BF16) for _ in range(2)]
            kd = ralloc([128, NCH + 1, 256], BF16)
            vtm = ralloc([128, NCH + 1, 256], BF16)
            sg = [ralloc([128, T], F32) for _ in range(2)]
            QDr = ralloc([128, T], F32)
            gout = [ralloc([128, T], BF16) for _ in range(2)]
            NB3 = 3
            NB2 = 2
            STm = [ralloc([128, 128], BF16) for _ in range(NB3)]
            S = [ralloc([128, 256], F32) for _ in range(2)]
            Sbf = [[ralloc([128, 256], BF16) for _ in range(2)] for _ in range(NB3)]
            sqt = [ralloc([128, 256], F32) for _ in range(NB2)]
            rst = [ralloc([128, 128], F32) for _ in range(NB2)]
            rs = [ralloc([128, 256], F32) for _ in range(NB2)]
            assert o[0] <= ARENA, o[0]
            keys = (["cos", "sin", "taq", "tak", "tb", "QDr", "kd", "vtm"] + [("qkv", i) for i in range(6)]
                    + [("qr", i) for i in range(2)] + [("qd", i) for i in range(2)] + [("kr", i) for i in range(2)]
                    + [("sg", i) for i in range(2)] + [("gout", i) for i in range(2)]
                    + [("STm", i) for i in range(NB3)] + [("S", i) for i in range(2)]
                    + [("Sbf", i, j) for i in range(NB3) for j in range(2)]
                    + [("sqt", i) for i in range(NB2)] + [("rst", i) for i in range(NB2)] + [("rs", i) for i in range(NB2)])
            enter_both(keys)
            P.op("sync", lambda e: [e.dma_start(out=cosT, in_=cs_in[0])], writes=["cos"], dma="cos")
            P.op("sync", lambda e: [e.dma_start(out=sinT, in_=cs_in[1])], writes=["sin"], dma="sin")
            TT = lambda out, a, b, op: (lambda e: e.tensor_tensor(out=out, in0=a, in1=b, op=op))
            NCK = len(CHUNKS)

            def head_loads(h):
                for i in range(6):
                    load(qkv[i], QKV[h, i], ("QKV", h, i), [("qkv", i)], f"qkv{i}")
                P.op("sync", lambda e, h=h: [e.dma_start(out=QDr, in_=qd_in[h])], writes=["QDr"], dma="QDr")

            def sg_loads(h):
                for j in range(2):
                    load(sg[j], SG[2 * h + j], ("SG", 2 * h + j), [("sg", j)], f"sg{j}")

            def rot(x0, x1, k0, k1, ta, tak_):
                P.op("vector", TT(ta, x0, cosT, ALU.mult), reads=[k0, "cos"], writes=[tak_])
                P.op("vector", TT(tb, x1, sinT, ALU.mult), reads=[k1, "sin"], writes=["tb"])
                P.op("vector", TT(ta, ta, tb, ALU.subtract), reads=[tak_, "tb"], writes=[tak_])
                P.op("vector", TT(tb, x0, sinT, ALU.mult), reads=[k0, "sin"], writes=["tb"])
                P.op("vector", TT(x1, x1, cosT, ALU.mult), reads=[k1, "cos"], writes=[k1])
                P.op("vector", TT(x1, x1, tb, ALU.add), reads=[k1, "tb"], writes=[k1])

            head_loads(0)
            sg_loads(0)
            for h in range(H):
                for ci, (t0, c) in enumerate(CHUNKS):
                    bk = banks(1)[0]

                    def trv(e, t0=t0, c=c, bk=bk):
                        e.transpose(out=ps[bk][:c, 0:128], in_=qkv[4][:, t0:t0 + c], identity=ident)
                        return e.transpose(out=ps[bk][:c, 128:256], in_=qkv[5][:, t0:t0 + c], identity=ident)
                    P.op("tensor", trv, reads=[("qkv", 4), ("qkv", 5), "ident"], writes=[("ps", bk)])
                    P.op("scalar", lambda e, ci=ci, c=c, bk=bk: e.activation(out=vtm[:c, ci, :], in_=ps[bk][:c, 0:256], func=AF.Copy),
                         reads=[("ps", bk)], writes=["vtm"])
                rot(qkv[2], qkv[3], ("qkv", 2), ("qkv", 3), tak, "tak")
                P.op("scalar", lambda e: e.activation(out=kr[0], in_=tak, func=AF.Copy), reads=["tak"], writes=[("kr", 0)])
                P.op("scalar", lambda e: e.activation(out=kr[1], in_=qkv[3], func=AF.Copy), reads=[("qkv", 3)], writes=[("kr", 1)])
                for ci, (t0, c) in enumerate(CHUNKS):
                    bk = banks(1)[0]

                    def trk(e, t0=t0, c=c, bk=bk):
                        e.transpose(out=ps[bk][:c, 0:128], in_=tak[:, t0:t0 + c], identity=ident)
                        return e.transpose(out=ps[bk][:c, 128:256], in_=qkv[3][:, t0:t0 + c], identity=ident)
                    P.op("tensor", trk, reads=["tak", ("qkv", 3), "ident"], writes=[("ps", bk)])
                    col = h * 2 + (0 if ci == 0 else 1)
                    P.op("vector", lambda e, ci=ci, c=c, bk=bk, col=col: e.tensor_scalar(
                        out=kd[:c, ci, :], in0=ps[bk][:c, 0:256], scalar1=kdt[:c, col:col + 1], scalar2=None, op0=ALU.mult),
                        reads=[("ps", bk), "kdt"], writes=["kd"])
                rot(qkv[0], qkv[1], ("qkv", 0), ("qkv", 1), taq, "taq")
                P.op("scalar", lambda e: e.activation(out=qr[0], in_=taq, func=AF.Copy), reads=["taq"], writes=[("qr", 0)])
                P.op("scalar", lambda e: e.activation(out=qr[1], in_=qkv[1], func=AF.Copy), reads=[("qkv", 1)], writes=[("qr", 1)])
                P.op("vector", TT(qd[0], taq, QDr, ALU.mult), reads=["taq", "QDr"], writes=[("qd", 0)])
                P.op("vector", TT(qd[1], qkv[1], QDr, ALU.mult), reads=[("qkv", 1), "QDr"], writes=[("qd", 1)])
                if h + 1 < H:
                    head_loads(h + 1)

                def emit_ST(ci, h=h):
                    t0, c = CHUNKS[ci]
                    bk = banks(1)[0]
                    sl = ci % NB3

                    def mm(e):
                        e.matmul(ps[bk][:c, :c], lhsT=kr[0][:, t0:t0 + c], rhs=qr[0][:, t0:t0 + c], start=True, stop=False)
                        return e.matmul(ps[bk][:c, :c], lhsT=kr[1][:, t0:t0 + c], rhs=qr[1][:, t0:t0 + c], start=False, stop=True)
                    P.op("tensor", mm, reads=[("kr", 0), ("kr", 1), ("qr", 0), ("qr", 1)], writes=[("ps", bk)])
                    P.op("vector", lambda e: e.tensor_tensor(out=STm[sl][:c, :c], in0=ps[bk][:c, :c], in1=maskT[:c, h, :c], op=ALU.mult),
                         reads=[("ps", bk), "maskT"], writes=[("STm", sl)])

                def emit_state(ci, h=h):
                    t0, c = CHUNKS[ci]
                    bk = banks(1)[0]
                    nsl = (ci + 1) % NB3

                    def mm(e):
                        e.matmul(ps[bk][:, 0:256], lhsT=kd[:c, ci, 0:128], rhs=vtm[:c, ci, :], start=True, stop=True)
                        return e.matmul(ps[bk][:, 256:512], lhsT=kd[:c, ci, 128:256], rhs=vtm[:c, ci, :], start=True, stop=True)
                    P.op("tensor", mm, reads=["kd", "vtm"], writes=[("ps", bk)])
                    for dh in range(2):
                        src = ps[bk][:, dh * 256:(dh + 1) * 256]
                        if ci == 0:
                            P.op("vector", lambda e, dh=dh, src=src: e.tensor_copy(out=S[dh], in_=src),
                                 reads=[("ps", bk)], writes=[("S", dh)])
                        else:
                            P.op("vector", lambda e, dh=dh, src=src: e.scalar_tensor_tensor(
                                out=S[dh], in0=S[dh], scalar=CD[h], in1=src, op0=ALU.mult, op1=ALU.add),
                                reads=[("ps", bk), ("S", dh)], writes=[("S", dh)])
                        P.op("scalar", lambda e, dh=dh: e.activation(out=Sbf[nsl][dh], in_=S[dh], func=AF.Copy),
                             reads=[("S", dh)], writes=[("Sbf", nsl, dh)])

                def emit_retA(ci):
                    t0, c = CHUNKS[ci]
                    bk = banks(1)[0]
                    sl3 = ci % NB3
                    sl = ci % NB2

                    def mm(e):
                        last = None
                        for dvh in range(2):
                            o_ = ps[bk][:, dvh * c:(dvh + 1) * c]
                            last = e.matmul(o_, lhsT=vtm[:c, ci, dvh * 128:(dvh + 1) * 128], rhs=STm[sl3][:c, :c],
                                            start=True, stop=(ci == 0))
                            if ci > 0:
                                for dh in range(2):
                                    last = e.matmul(o_, lhsT=Sbf[sl3][dh][:, dvh * 128:(dvh + 1) * 128], rhs=qd[dh][:, t0:t0 + c],
                                                    start=False, stop=(dh == 1))
                        return last
                    rd = ["vtm", ("STm", sl3)] + ([("Sbf", sl3, 0), ("Sbf", sl3, 1), ("qd", 0), ("qd", 1)] if ci > 0 else [])
                    P.op("tensor", mm, reads=rd, writes=[("ps", bk)])
                    P.op("scalar", lambda e: e.activation(out=sqt[sl][:, :2 * c], in_=ps[bk][:, :2 * c], func=AF.Square),
                         reads=[("ps", bk)], writes=[("sqt", sl)])
                    return bk

                def emit_retB(ci, bk):
                    t0, c = CHUNKS[ci]
                    sl = ci % NB2
                    bn = banks(1)[0]

                    def mm2(e):
                        e.matmul(ps[bn][:, :c], lhsT=ones256, rhs=sqt[sl][:, 0:c], start=True, stop=False)
                        return e.matmul(ps[bn][:, :c], lhsT=ones256, rhs=sqt[sl][:, c:2 * c], start=False, stop=True)
                    P.op("tensor", mm2, reads=[("sqt", sl), "ones256"], writes=[("ps", bn)])
                    P.op("scalar", lambda e: e.activation(out=rst[sl][:, :c], in_=ps[bn][:, :c], func=AF.Sqrt, bias=EPS, scale=1.0),
                         reads=[("ps", bn)], writes=[("rst", sl)])
                    P.op("vector", lambda e: e.reciprocal(out=rst[sl][:, :c], in_=rst[sl][:, :c]), reads=[("rst", sl)], writes=[("rst", sl)])
                    for dvh in range(2):
                        P.op("vector", lambda e, dvh=dvh: e.tensor_tensor(out=rs[sl][:, dvh * c:(dvh + 1) * c], in0=rst[sl][:, :c],
                                                                          in1=sg[dvh][:, t0:t0 + c], op=ALU.mult),
                             reads=[("rst", sl), ("sg", dvh)], writes=[("rs", sl)])
                    for dvh in range(2):
                        P.op("vector", lambda e, dvh=dvh: e.tensor_tensor(out=gout[dvh][:, t0:t0 + c], in0=ps[bk][:, dvh * c:(dvh + 1) * c],
                                                                          in1=rs[sl][:, dvh * c:(dvh + 1) * c], op=ALU.mult),
                             reads=[("ps", bk), ("rs", sl)], writes=[("gout", dvh)])

                emit_ST(0)
                if NCK > 1:
                    emit_ST(1)
                    emit_state(0)
                for ci in range(NCK):
                    bk = emit_retA(ci)
                    if ci + 2 < NCK:
                        emit_ST(ci + 2)
                    if ci + 2 < NCK:
                        emit_state(ci + 1)
                    emit_retB(ci, bk)
                for dvh in range(2):
                    store(GT[2 * h + dvh], gout[dvh], [("gout", dvh)], ("GT", 2 * h + dvh), f"gout{dvh}")
                if h + 1 < H:
                    sg_loads(h + 1)

        def phase_P3():
            enter_phase("A", AKEYS)
            a_load([GT[kc] for kc in range(RC)] + [YB[kc] for kc in range(CC)],
                   [("GT", kc) for kc in range(RC)] + [("YB", kc) for kc in range(CC)])
            o = X_OFF
            gr = []
            for i in range(3):
                gr.append(view(o, [128, T], F32)); o += ROW4
            tmp = []
            for i in range(4):
                tmp.append(view(o, [128, 512], F32)); o += 2048
            mrow = []
            for i in range(2):
                mrow.append(view(o, [128, T], BF16)); o += ROW2
            keys = [("gr", i) for i in range(3)] + [("tmp", i) for i in range(4)] + GK(("mrow", 0)) + GK(("mrow", 1))
            enter_phase("X", keys)
            grc = [0]
            tc = [0]
            pre = {}

            def prefetch(m):
                if m >= KC or m in pre:
                    return
                ia = grc[0] % 3
                ib = (grc[0] + 1) % 3
                grc[0] += 2
                load(gr[ia], GA[m], ("GA", m), [("gr", ia)], f"gr{ia}")
                load(gr[ib], GB[m], ("GB", m), [("gr", ib)], f"gr{ib}")
                pre[m] = (ia, ib)
            prefetch(0)
            for m in range(KC):
                ia, ib = pre[m]
                mi = m % 2
                mr = mrow[mi]

                def epi(g, off, n, bks, ia=ia, ib=ib, mr=mr, mi=mi):
                    t1 = tc[0] % 4
                    t2 = (tc[0] + 1) % 4
                    tc[0] += 2
                    P.op("vector", lambda e: e.tensor_tensor(out=tmp[t1][:, :n], in0=ps[bks[0]][:, :n], in1=gr[ia][:, off:off + n], op=ALU.mult),
                         reads=[("ps", bks[0]), ("gr", ia)], writes=[("tmp", t1)])
                    P.op("vector", lambda e: e.tensor_tensor(out=tmp[t2][:, :n], in0=ps[bks[1]][:, :n], in1=gr[ib][:, off:off + n], op=ALU.mult),
                         reads=[("ps", bks[1]), ("gr", ib)], writes=[("tmp", t2)])
                    P.op("vector", lambda e: e.tensor_tensor(out=mr[:, off:off + n], in0=tmp[t1][:, :n], in1=tmp[t2][:, :n], op=ALU.add),
                         reads=[("tmp", t1), ("tmp", t2)], writes=[(("mrow", mi), g)])
                lin_block(epi, kc_split=[list(range(RC)), list(range(RC, KC))], edge=(m == 0 or m == KC - 1))
                if m + 1 < KC:
                    prefetch(m + 1)
                store(MG[m], mr, GK(("mrow", mi)), ("MG", m), f"mrow{mi}")

        def phase_resid(src_dram, src_name, blk0, stats=False):
            enter_phase("A", AKEYS)
            a_load([src_dram[blk0 + kc] for kc in range(KC)], [(src_name, blk0 + kc) for kc in range(KC)])
            o = X_OFF
            xr = []
            for i in range(2):
                xr.append(view(o, [128, T], F32)); o += ROW4
            hr = []
            for i in range(2):
                hr.append(view(o, [128, T], F32)); o += ROW4
            acc = view(RSTD_OFF, [128, T], F32)
            keys = [("xr", 0), ("xr", 1)] + GK(("hr", 0)) + GK(("hr", 1))
            enter_phase("X", keys + (["rstd"] if stats else []))
            load(xr[0], R[0], ("R", 0), [("xr", 0)], "xr0")
            for m in range(KC):
                mi = m % 2
                if m + 1 < KC:
                    load(xr[(m + 1) % 2], R[m + 1], ("R", m + 1), [("xr", (m + 1) % 2)], f"xr{(m + 1) % 2}")

                def epi(g, off, n, bks, mi=mi):
                    P.op("vector", lambda e: e.tensor_tensor(out=hr[mi][:, off:off + n], in0=ps[bks[0]][:, :n],
                                                             in1=xr[mi][:, off:off + n], op=ALU.add),
                         reads=[("ps", bks[0]), ("xr", mi)], writes=[(("hr", mi), g)])
                lin_block(epi, edge=(m == 0 or m == KC - 1))
                store(R[m], hr[mi], GK(("hr", mi)), ("R", m), f"hr{mi}")
                if stats:
                    P.op("scalar", lambda e, mi=mi: e.activation(out=xr[mi], in_=hr[mi], func=AF.Square),
                         reads=GK(("hr", mi)), writes=[("xr", mi)])
                    if m == 0:
                        P.op("vector", lambda e, mi=mi: e.tensor_copy(out=acc, in_=xr[mi]), reads=[("xr", mi)], writes=["rstd"])
                    else:
                        P.op("vector", lambda e, mi=mi: e.tensor_tensor(out=acc, in0=acc, in1=xr[mi], op=ALU.add),
                             reads=[("xr", mi), "rstd"], writes=["rstd"])
            if stats:
                stats_finish(acc)

        def phase_P5():
            o = X_OFF
            tmp = []
            for i in range(4):
                tmp.append(view(o, [128, 512], F32)); o += 2048
            hrow = []
            for i in range(2):
                hrow.append(view(o, [128, T], BF16)); o += ROW2
            keys = [("tmp", i) for i in range(4)] + GK(("hrow", 0)) + GK(("hrow", 1))
            enter_phase("X", keys)
            tc = [0]
            for f in range(FC):
                hi = f % 2

                def epi(g, off, n, bks, hi=hi):
                    t1 = tc[0] % 4
                    tc[0] += 1
                    P.op("scalar", lambda e: e.activation(out=tmp[t1][:, :n], in_=ps[bks[0]][:, :n], func=AF.Relu),
                         reads=[("ps", bks[0])], writes=[("tmp", t1)])
                    P.op("vector", lambda e: e.tensor_tensor(out=hrow[hi][:, off:off + n], in0=tmp[t1][:, :n], in1=tmp[t1][:, :n], op=ALU.mult),
                         reads=[("tmp", t1)], writes=[(("hrow", hi), g)])
                lin_block(epi, edge=(f == 0 or f == FC - 1))
                store(HID[f], hrow[hi], GK(("hrow", hi)), ("HID", f), f"hrow{hi}")

        def phase_final():
            _, _, rstd = phase_norm(2 * L, have_stats=True, final=True)
            KQ = 4
            need = 2 * KQ * ROW4 + 2 * al(NCH * KQ * 128 * 4)
            if A_OFF + need <= RSTD_OFF:
                fo = A_OFF
            else:
                fo = RSTD_OFF + ROW4
                assert fo + need <= ARENA, (fo, need)
            xq = [[view(fo + (b * KQ + j) * ROW4, [128, T], F32) for j in range(KQ)] for b in range(2)]
            fo2 = fo + 2 * KQ * ROW4
            ost = [view(fo2 + b * al(NCH * KQ * 128 * 4), [128, NCH, KQ * 128], F32) for b in range(2)]
            keys = [("xq", 0), ("xq", 1), ("ost", 0), ("ost", 1)]
            P.alias(keys, [k for k in region_keys["A"] + region_keys["X"] if k != "rstd"])
            gi = 2 * L
            toks = []
            for kq in range(KC // KQ):
                b = kq % 2
                P.op("sync", lambda e, kq=kq, b=b: [e.dma_start(out=xq[b][j], in_=R[kq * KQ + j]) for j in range(KQ)],
                     reads=[("R", kq * KQ + j) for j in range(KQ)], writes=[("xq", b)], dma=f"xq{b}", ndma=KQ)
                for j in range(KQ):
                    kc = kq * KQ + j
                    P.op("vector", lambda e, b=b, j=j, kc=kc: e.scalar_tensor_tensor(
                        out=xq[b][j], in0=xq[b][j], scalar=gains[:, gi, kc:kc + 1], in1=rstd, op0=ALU.mult, op1=ALU.mult),
                        reads=[("xq", b), "rstd", "gains"], writes=[("xq", b)])
                for c in range(NCH):
                    bk = banks(1)[0]
                    t0 = T0 + c * 128

                    def tr(e, b=b, t0=t0, bk=bk):
                        last = None
                        for j in range(KQ):
                            last = e.transpose(out=ps[bk][:, j * 128:(j + 1) * 128], in_=xq[b][j][:, t0:t0 + 128], identity=ident)
                        return last
                    P.op("tensor", tr, reads=[("xq", b), "ident"], writes=[("ps", bk)])
                    dst = ost[b][:, c, :]
                    srcp = ps[bk][:, :KQ * 128]
                    if c % 2 == 0:
                        P.op("vector", lambda e, dst=dst, srcp=srcp: e.tensor_copy(out=dst, in_=srcp),
                             reads=[("ps", bk)], writes=[("ost", b)])
                    else:
                        P.op("scalar", lambda e, dst=dst, srcp=srcp: e.activation(out=dst, in_=srcp, func=AF.Copy),
                             reads=[("ps", bk)], writes=[("ost", b)])
                dsty = y_out[:, kq * KQ * 128:(kq + 1) * KQ * 128].rearrange("(c p) f -> p c f", p=128)
                t = P.op("gpsimd", lambda e, dsty=dsty, b=b: [e.dma_start(out=dsty, in_=ost[b])],
                         reads=[("ost", b)], writes=[("y", kq)], dma=f"ost{b}")
                toks.append(t)
            P.wait_all("gpsimd", toks)

        stages = [("T0", phase_T0)]
        for l in range(L):
            stages += [("N1", lambda l=l: phase_norm(2 * l, have_stats=(l > 0))), ("P1", lambda l=l: phase_P1(l)), ("P2", phase_P2),
                       ("P3", phase_P3), ("P4", lambda: phase_resid(MG, "MG", 0, stats=True)),
                       ("N2", lambda l=l: phase_norm(2 * l + 1, have_stats=True)), ("P5", phase_P5)]
            for qd in range(NQ):
                stages.append(("P6", lambda qd=qd: phase_resid(HID, "HID", qd * KC, stats=(qd == NQ - 1))))
        stages.append(("F", phase_final))
        if limit is not None:
            stages = stages[:limit]
        for nm, fn in stages:
            fn()
        if limit is None:
            assert wstate["next_use"] == NTOT, (wstate, NTOT)
        P.emit()
    return nc


_CACHE = {}


def run(cfg, x, meta_tokens, norm_mix_g, w_in, conv_w, w_ret_out, w_conv_out, w_out,
        norm_mlp_g, w_up, w_down, final_norm_g, n_cores=None):
    x = np.asarray(x)
    B = x.shape[0]
    n_cores = B if n_cores is None else n_cores
    key = (cfg.D, cfg.H, cfg.DFF, cfg.NCH, cfg.depth)
    if key not in _CACHE:
        _CACHE[key] = build(cfg)
    nc = _CACHE[key]
    ct = const_tables(cfg)
    gains, cw = prep_small(cfg, norm_mix_g, norm_mlp_g, final_norm_g, conv_w)
    shared = {
        "meta": np.ascontiguousarray(np.asarray(meta_tokens, dtype=np.float32)),
        "gains": gains.reshape(128, -1), "cw": cw.reshape(128, -1),
        "cs": ct["cs"], "maskT": ct["maskT"].reshape(128, -1), "QD": ct["QD"], "KD": ct["KD"].reshape(128, -1),
        "ident": ct["ident"],
    }
    for l in range(cfg.depth):
        shared[f"w{l}"] = prep_weights(cfg, l, w_in, w_ret_out, w_conv_out, w_out, w_up, w_down)
    in_maps = []
    for b in range(n_cores):
        m = dict(shared)
        m["x"] = np.ascontiguousarray(x[b])
        in_maps.append(m)
    res = run_bass_kernel_spmd(nc, in_maps, core_ids=list(range(n_cores)))
    return np.stack([np.asarray(r["y"]) for r in res.results], axis=0).astype(np.float32)


def kernel(x, meta_tokens, norm_mix_g, w_in, conv_w, w_ret_out, w_conv_out, w_out,
           norm_mlp_g, w_up, w_down, final_norm_g):
    cfg = Cfg(**FULL)
    return run(cfg, x, meta_tokens, norm_mix_g, w_in, conv_w, w_ret_out, w_conv_out, w_out,
               norm_mlp_g, w_up, w_down, final_norm_g)
```
